# Optimizing a Trainium2 kernel written in Bass

```python
import jax, jax.numpy as jnp
from jax import lax
import numpy as np

D_MODEL = 2048
BATCH = 8
SEQ = 2048
DEPTH = 4
DEC_BATCH = 32
DEC_SEQ = 32
PAST_LEN = 4096

CHUNK = 64
D_POOL = 1024
POOL_WINDOWS = (2, 4, 8, 16)
N_POOL_GROUPS = 4
POOL_GROUP = D_POOL // N_POOL_GROUPS
POOL_HIST = max(POOL_WINDOWS) - 1
D_SSD = 2048
SSD_HEAD_DIM = 64
SSD_HEADS = D_SSD // SSD_HEAD_DIM
SSD_GROUPS = 2
HEADS_PER_GROUP = SSD_HEADS // SSD_GROUPS
D_STATE = 128
SSD_CONV = 4
CONV_DIM = D_SSD + 2 * SSD_GROUPS * D_STATE
D_MIX = D_POOL + D_SSD
D_IN_PROJ = D_POOL + D_SSD + CONV_DIM + SSD_HEADS
D_FF = 5632
FFN_CONV = 3
EPS = 1e-6

kernel_name = "pool_ssd_hybrid_streaming_step"


def rmsnorm(x, g):
    xf = x.astype(jnp.float32)
    y = xf * lax.rsqrt(jnp.mean(xf * xf, axis=-1, keepdims=True) + EPS)
    return (y * g.astype(jnp.float32)).astype(x.dtype)


def causal_dwconv(x, hist, w, b):
    k = w.shape[0]
    s = x.shape[1]
    ext = jnp.concatenate([hist.astype(x.dtype), x], axis=1)
    out = ext[:, 0:s] * w[0]
    for i in range(1, k):
        out = out + ext[:, i:i + s] * w[i]
    return out + b, ext[:, -(k - 1):]


def pool_mixer(u, hist, pos, w_pool, pool_scale):
    s = u.shape[1]
    ext = jnp.concatenate([hist.astype(u.dtype), u], axis=1)
    cs = jnp.cumsum(ext.astype(jnp.float32), axis=1)
    cs = jnp.pad(cs, ((0, 0), (1, 0), (0, 0)))
    end = cs[:, POOL_HIST + 1:]
    means = []
    for gi, w in enumerate(POOL_WINDOWS):
        sl = slice(gi * POOL_GROUP, (gi + 1) * POOL_GROUP)
        start = cs[:, POOL_HIST + 1 - w:POOL_HIST + 1 - w + s, sl]
        cnt = jnp.minimum(pos + 1, w).astype(jnp.float32)[None, :, None]
        means.append((end[..., sl] - start) / cnt)
    mean = jnp.concatenate(means, axis=-1)
    d = (mean - u.astype(jnp.float32)).astype(u.dtype)
    d = d.reshape(u.shape[0], s, N_POOL_GROUPS, POOL_GROUP)
    y = jnp.einsum('bsgc,gcd->bsgd', d, w_pool).reshape(u.shape)
    return y * pool_scale, ext[:, -POOL_HIST:]


def ssd_scan(xh, dt, A, Bm, Cm, D, h0):
    b, s = xh.shape[0], xh.shape[1]
    q = min(CHUNK, s)
    c = s // q
    G, J, P, N = SSD_GROUPS, HEADS_PER_GROUP, SSD_HEAD_DIM, D_STATE
    xc = (xh * dt[..., None]).reshape(b, c, q, G, J, P)
    Bc = Bm.reshape(b, c, q, G, N)
    Cc = Cm.reshape(b, c, q, G, N)
    a = jnp.moveaxis((dt * A).reshape(b, c, q, G, J), 2, -1)
    acum = jnp.cumsum(a, axis=-1)
    diff = acum[..., :, None] - acum[..., None, :]
    causal = jnp.tril(jnp.ones((q, q), dtype=bool))
    lmat = jnp.exp(jnp.where(causal, diff, -jnp.inf))
    cb = jnp.einsum('bclgn,bcsgn->bcgls', Cc, Bc)
    y_diag = jnp.einsum('bcgls,bcgjls,bcsgjp->bclgjp', cb, lmat, xc)
    decay_states = jnp.exp(acum[..., -1:] - acum)
    states = jnp.einsum('bclgn,bcgjl,bclgjp->bcgjpn', Bc, decay_states, xc)
    chunk_decay = jnp.exp(acum[..., -1])

    def step(h, inp):
        dec, st = inp
        return dec[..., None, None] * h + st, h

    h_last, h_prev = lax.scan(step, h0, (jnp.moveaxis(chunk_decay, 1, 0), jnp.moveaxis(states, 1, 0)))
    h_prev = jnp.moveaxis(h_prev, 0, 1)
    y_off = jnp.einsum('bclgn,bcgjpn,bcgjl->bclgjp', Cc, h_prev, jnp.exp(acum))
    y = (y_diag + y_off).reshape(b, s, G, J, P) + D[..., None] * xh
    return y, h_last


def ssd_mixer(z, xbc, dt_raw, conv_hist, h0, conv_w, conv_b, dt_bias, a_log, d_skip, norm_g):
    b, s = z.shape[0], z.shape[1]
    G, J, P, N = SSD_GROUPS, HEADS_PER_GROUP, SSD_HEAD_DIM, D_STATE
    xbc, new_conv = causal_dwconv(xbc, conv_hist, conv_w, conv_b)
    xbc = jax.nn.silu(xbc).astype(jnp.float32)
    xs = xbc[..., :D_SSD].reshape(b, s, G, J, P)
    Bm = xbc[..., D_SSD:D_SSD + G * N].reshape(b, s, G, N)
    Cm = xbc[..., D_SSD + G * N:].reshape(b, s, G, N)
    dt = jax.nn.softplus(dt_raw.astype(jnp.float32) + dt_bias.astype(jnp.float32)).reshape(b, s, G, J)
    A = -jnp.exp(a_log.astype(jnp.float32)).reshape(G, J)
    h0f = h0.astype(jnp.float32).reshape(b, G, J, P, N)
    y, h_last = ssd_scan(xs, dt, A, Bm, Cm, d_skip.astype(jnp.float32).reshape(G, J), h0f)
    y = y * jax.nn.silu(z.astype(jnp.float32)).reshape(b, s, G, J, P)
    yg = y.reshape(b, s, G, J * P)
    yg = yg * lax.rsqrt(jnp.mean(yg * yg, axis=-1, keepdims=True) + EPS)
    yg = yg * norm_g.astype(jnp.float32).reshape(G, J * P)
    y_out = yg.reshape(b, s, D_SSD).astype(z.dtype)
    return y_out, new_conv, h_last.reshape(b, SSD_HEADS, P, N).astype(h0.dtype)


def run_layer(x, pos, hist_pool, hist_sconv, h0, hist_fconv, p):
    xn = rmsnorm(x, p['norm_mix_pre'])
    proj = xn @ p['w_in']
    o1 = D_POOL
    o2 = o1 + D_SSD
    o3 = o2 + CONV_DIM
    u, z, xbc, dt_raw = proj[..., :o1], proj[..., o1:o2], proj[..., o2:o3], proj[..., o3:]
    y_pool, new_pool = pool_mixer(u, hist_pool, pos, p['w_pool'], p['pool_scale'])
    y_ssd, new_sconv, new_h = ssd_mixer(z, xbc, dt_raw, hist_sconv, h0, p['ssd_conv_w'], p['ssd_conv_b'],
                                        p['ssd_dt_bias'], p['ssd_a_log'], p['ssd_d'], p['ssd_norm'])
    mix = jnp.concatenate([y_pool, y_ssd], axis=-1) @ p['w_out']
    x = x + rmsnorm(mix, p['norm_mix_post'])
    hn = rmsnorm(x, p['norm_ffn_pre'])
    up = hn @ p['w_up']
    up, new_fconv = causal_dwconv(up, hist_fconv, p['ffn_conv_w'], p['ffn_conv_b'])
    gate, val = up[..., :D_FF], up[..., D_FF:]
    f = (jax.nn.silu(gate) * val) @ p['w_down']
    x = x + rmsnorm(f, p['norm_ffn_post'])
    return x, new_pool, new_sconv, new_h, new_fconv


def setup_inputs(seed: int = 0) -> dict:
    key = jax.random.key(seed)
    ks = jax.random.split(key, 32)
    f32 = jnp.float32
    nrm = lambda k, shape, sc: jax.random.normal(k, shape, f32) * sc
    gain = lambda k, shape: 1.0 + 0.1 * jax.random.normal(k, shape, f32)
    dt0 = jnp.exp(jax.random.uniform(ks[10], (DEPTH, SSD_HEADS), f32, np.log(1e-3), np.log(1e-1)))
    return {
        "x_prompt": nrm(ks[0], (BATCH, SEQ, D_MODEL), 1.0),
        "x_sample": nrm(ks[1], (DEC_BATCH, DEC_SEQ, D_MODEL), 1.0),
        "cache_pool": nrm(ks[2], (DEPTH, DEC_BATCH, POOL_HIST, D_POOL), 1.0),
        "state_ssd_conv": nrm(ks[3], (DEPTH, DEC_BATCH, SSD_CONV - 1, CONV_DIM), 1.0),
        "state_ssd": nrm(ks[4], (DEPTH, DEC_BATCH, SSD_HEADS, SSD_HEAD_DIM, D_STATE), 0.1),
        "state_ffn_conv": nrm(ks[5], (DEPTH, DEC_BATCH, FFN_CONV - 1, 2 * D_FF), 1.0),
        "norm_mix_pre": gain(ks[6], (DEPTH, D_MODEL)),
        "w_in": nrm(ks[7], (DEPTH, D_MODEL, D_IN_PROJ), D_MODEL ** -0.5),
        "w_pool": nrm(ks[8], (DEPTH, N_POOL_GROUPS, POOL_GROUP, POOL_GROUP), POOL_GROUP ** -0.5),
        "pool_scale": gain(ks[9], (DEPTH, D_POOL)),
        "ssd_conv_w": nrm(ks[11], (DEPTH, SSD_CONV, CONV_DIM), SSD_CONV ** -0.5),
        "ssd_conv_b": nrm(ks[12], (DEPTH, CONV_DIM), 0.02),
        "ssd_dt_bias": dt0 + jnp.log(-jnp.expm1(-dt0)),
        "ssd_a_log": jnp.log(jax.random.uniform(ks[13], (DEPTH, SSD_HEADS), f32, 1.0, 16.0)),
        "ssd_d": gain(ks[14], (DEPTH, SSD_HEADS)),
        "ssd_norm": gain(ks[15], (DEPTH, D_SSD)),
        "w_out": nrm(ks[16], (DEPTH, D_MIX, D_MODEL), D_MIX ** -0.5),
        "norm_mix_post": gain(ks[17], (DEPTH, D_MODEL)),
        "norm_ffn_pre": gain(ks[18], (DEPTH, D_MODEL)),
        "w_up": nrm(ks[19], (DEPTH, D_MODEL, 2 * D_FF), D_MODEL ** -0.5),
        "ffn_conv_w": nrm(ks[20], (DEPTH, FFN_CONV, 2 * D_FF), FFN_CONV ** -0.5),
        "ffn_conv_b": nrm(ks[21], (DEPTH, 2 * D_FF), 0.02),
        "w_down": nrm(ks[22], (DEPTH, D_FF, D_MODEL), D_FF ** -0.5),
        "norm_ffn_post": gain(ks[23], (DEPTH, D_MODEL)),
    }


def reference(x_prompt, x_sample, cache_pool, state_ssd_conv, state_ssd, state_ffn_conv,
              norm_mix_pre, w_in, w_pool, pool_scale, ssd_conv_w, ssd_conv_b, ssd_dt_bias,
              ssd_a_log, ssd_d, ssd_norm, w_out, norm_mix_post, norm_ffn_pre, w_up,
              ffn_conv_w, ffn_conv_b, w_down, norm_ffn_post):
    bp, sp = x_prompt.shape[0], x_prompt.shape[1]
    sd = x_sample.shape[1]
    dtp = x_prompt.dtype
    pos_p = jnp.arange(sp, dtype=jnp.int32)
    pos_s = PAST_LEN + jnp.arange(sd, dtype=jnp.int32)
    zp_pool = jnp.zeros((bp, POOL_HIST, D_POOL), dtp)
    zp_sconv = jnp.zeros((bp, SSD_CONV - 1, CONV_DIM), dtp)
    zp_h = jnp.zeros((bp, SSD_HEADS, SSD_HEAD_DIM, D_STATE), state_ssd.dtype)
    zp_fconv = jnp.zeros((bp, FFN_CONV - 1, 2 * D_FF), dtp)

    xp, xs = x_prompt, x_sample
    pool_p, pool_s, sconv_p, sconv_s, h_p, h_s, fconv_p, fconv_s = [], [], [], [], [], [], [], []
    for l in range(DEPTH):
        p = dict(norm_mix_pre=norm_mix_pre[l], w_in=w_in[l], w_pool=w_pool[l], pool_scale=pool_scale[l],
                 ssd_conv_w=ssd_conv_w[l], ssd_conv_b=ssd_conv_b[l], ssd_dt_bias=ssd_dt_bias[l],
                 ssd_a_log=ssd_a_log[l], ssd_d=ssd_d[l], ssd_norm=ssd_norm[l], w_out=w_out[l],
                 norm_mix_post=norm_mix_post[l], norm_ffn_pre=norm_ffn_pre[l], w_up=w_up[l],
                 ffn_conv_w=ffn_conv_w[l], ffn_conv_b=ffn_conv_b[l], w_down=w_down[l],
                 norm_ffn_post=norm_ffn_post[l])
        xp, a1, a2, a3, a4 = run_layer(xp, pos_p, zp_pool, zp_sconv, zp_h, zp_fconv, p)
        xs, b1, b2, b3, b4 = run_layer(xs, pos_s, cache_pool[l], state_ssd_conv[l], state_ssd[l],
                                       state_ffn_conv[l], p)
        pool_p.append(a1); sconv_p.append(a2); h_p.append(a3); fconv_p.append(a4)
        pool_s.append(b1); sconv_s.append(b2); h_s.append(b3); fconv_s.append(b4)

    return (xp, xs,
            jnp.stack(pool_p), jnp.stack(pool_s),
            jnp.stack(sconv_p), jnp.stack(sconv_s),
            jnp.stack(h_p), jnp.stack(h_s),
            jnp.stack(fconv_p), jnp.stack(fconv_s))
```

```python
import contextlib
import numpy as np
import concourse.bass as bass
import concourse.mybir as mybir
from concourse.bass_utils import run_bass_kernel_spmd

F32 = mybir.dt.float32
BF16 = mybir.dt.bfloat16
AF = mybir.ActivationFunctionType
ALU = mybir.AluOpType

DEPTH = 4
DM = 2048
DPOOL = 1024
DSSD = 2048
CONV = 2560
DFF = 5632
DIN = 5664
EPS = 1e-6
TP = 512
TS = 32
W = TP + TS
NKD = DM // 128
COMPUTE = ("pe", "act", "dve", "pool")


class Op:
    __slots__ = ("eng", "fn", "deps", "inc", "val", "dma", "sem_i")

    def __init__(self, eng, fn, dma):
        self.eng = eng
        self.fn = fn
        self.deps = []
        self.inc = False
        self.val = 0
        self.dma = dma
        self.sem_i = 0


class _Rec:
    def __init__(self):
        self.call = None

    def __getattr__(self, name):
        def f(*a, **k):
            self.call = (name, a, k)
            return self
        return f


class Prog:
    NDMA_SEM = 8

    def __init__(self):
        self.ops = {e: [] for e in ("pe", "act", "dve", "pool", "sp")}
        self.last_w = {}
        self.readers = {}

    def op(self, eng, fn, reads=(), writes=(), dma=False, sreads=()):
        rec = _Rec()
        fn(rec)
        o = Op(eng, rec.call, dma)
        deps = {}

        def add(d, force=False):
            if d is None or d is o:
                return
            if d.eng == o.eng and not d.dma and not o.dma and not force:
                return
            deps[id(d)] = d

        for k in reads:
            add(self.last_w.get(k), force=(eng != "pe"))
        for k in sreads:
            add(self.last_w.get(k), force=True)
        for k in writes:
            add(self.last_w.get(k))
            for r in self.readers.get(k, ()):
                add(r)
        for k in list(reads) + list(sreads):
            self.readers.setdefault(k, []).append(o)
        for k in writes:
            self.last_w[k] = o
            self.readers[k] = []
        o.deps = list(deps.values())
        self.ops[eng].append(o)
        return o

    def emit(self, nc, stack):
        for e, lst in self.ops.items():
            for o in lst:
                for d in o.deps:
                    d.inc = True
        sems = {e: stack.enter_context(nc.semaphore("s_" + e)) for e in COMPUTE}
        dsems = {e: [stack.enter_context(nc.semaphore("d_%s%d" % (e, i))) for i in range(self.NDMA_SEM)]
                 for e in ("sp", "pool", "act")}
        for e, lst in self.ops.items():
            cnt = 0
            dcnt = 0
            for o in lst:
                if o.dma:
                    o.sem_i = dcnt % self.NDMA_SEM
                    o.val = 16 * (dcnt // self.NDMA_SEM + 1)
                    dcnt += 1
                elif o.inc:
                    cnt += 1
                    o.val = cnt
        engs = {"pe": "tensor", "act": "scalar", "dve": "vector", "pool": "gpsimd", "sp": "sync"}
        block = stack.enter_context(nc.Block())

        def run(e, handle):
            waited = {}
            pend = [None] * self.NDMA_SEM

            def wait_dma(p):
                key = ("d", p.eng, p.sem_i)
                if waited.get(key, 0) < p.val:
                    handle.wait_ge(dsems[p.eng][p.sem_i], p.val)
                    waited[key] = p.val

            for o in self.ops[e]:
                if o.dma:
                    if pend[o.sem_i] is not None:
                        wait_dma(pend[o.sem_i])
                    pend[o.sem_i] = o
                need = {}
                for d in o.deps:
                    key = ("d", d.eng, d.sem_i) if d.dma else ("c", d.eng)
                    if key not in need or need[key].val < d.val:
                        need[key] = d
                for key, d in need.items():
                    if d.dma:
                        wait_dma(d)
                    elif waited.get(key, 0) < d.val:
                        handle.wait_ge(sems[d.eng], d.val)
                        waited[key] = d.val
                name_, a_, k_ = o.fn
                ins = getattr(handle, name_)(*a_, **k_)
                if o.dma:
                    ins.then_inc(dsems[e][o.sem_i], 16)
                elif o.inc:
                    ins.then_inc(sems[e], 1)
            for p in pend:
                if p is not None:
                    wait_dma(p)

        for e in ("sp", "pool", "act", "dve", "pe"):
            if self.ops[e]:
                getattr(block, engs[e])(lambda h, e=e: run(e, h))


class Rot:
    def __init__(self, name, aps):
        self.name = name
        self.aps = aps
        self.i = 0

    def next(self):
        i = self.i
        self.i = (i + 1) % len(self.aps)
        return self.aps[i], (self.name, i)


def make_consts():
    c = {}
    c["identf"] = np.eye(128, dtype=np.float32)
    k = np.arange(128)
    same64 = (k[:, None] // 64) == (k[None, :] // 64)
    c["tri64"] = (same64 & (k[:, None] <= k[None, :])).astype(np.float32)
    c["rest64"] = (same64 & (k[:, None] > k[None, :])).astype(np.float32)
    c["blk64"] = same64.astype(np.float32)
    c["sel0"] = np.repeat((k < 64).astype(np.float32)[:, None], 128, 1)
    c["sel1"] = np.repeat((k >= 64).astype(np.float32)[:, None], 128, 1)
    same32 = (k[:, None] < 32) & (k[None, :] < 32)
    c["tri32"] = (same32 & (k[:, None] <= k[None, :])).astype(np.float32)
    c["rest32"] = (same32 & (k[:, None] > k[None, :])).astype(np.float32)
    c["blk32"] = same32.astype(np.float32)
    c["sel32"] = np.repeat((k < 32).astype(np.float32)[:, None], 128, 1)
    lp = np.arange(64)
    c["triu"] = ((k[:, None] % 64) <= lp[None, :]).astype(np.float32)
    c["negm"] = np.where(lp[None, :] >= (k[:, None] % 64), 0.0, -30000.0).astype(np.float32)
    return c


CONST_NAMES = ["identf", "tri64", "rest64", "blk64", "sel0", "sel1", "tri32", "rest32", "blk32", "sel32"]


def build_nc(NT=4, NL=DEPTH):
    nc = bass.Bass("TRN2", target_bir_lowering=False)
    P = Prog()

    def din(name, shape):
        return nc.dram_tensor(name, list(shape), F32, kind="ExternalInput").ap()

    def dout(name, shape):
        return nc.dram_tensor(name, list(shape), F32, kind="ExternalOutput").ap()

    xp = din("xp", [2048, DM])
    xs = din("xs", [4, TS, DM])
    cpool = din("cpool", [DEPTH, 4, 15, DPOOL])
    csconv = din("csconv", [DEPTH, 4, 3, CONV])
    cssd = din("cssd", [DEPTH, 4, 2048, 128])
    cfconv = din("cfconv", [DEPTH, 4, 2, 2 * DFF])
    g_mix_pre = din("norm_mix_pre", [DEPTH, DM])
    w_in = din("w_in", [DEPTH, DM, DIN])
    w_pool = din("w_pool", [DEPTH, 4, 256, 256])
    pool_scale = din("pool_scale", [DEPTH, DPOOL])
    ssd_conv_w = din("ssd_conv_w", [DEPTH, 4, CONV])
    ssd_conv_b = din("ssd_conv_b", [DEPTH, CONV])
    ssd_dt_bias = din("ssd_dt_bias", [DEPTH, 32])
    ssd_a_log = din("ssd_a_log", [DEPTH, 32])
    ssd_d = din("ssd_d", [DEPTH, 32])
    ssd_norm = din("ssd_norm", [DEPTH, DSSD])
    w_out = din("w_out", [DEPTH, 3072, DM])
    g_mix_post = din("norm_mix_post", [DEPTH, DM])
    g_ffn_pre = din("norm_ffn_pre", [DEPTH, DM])
    w_up = din("w_up", [DEPTH, DM, 2 * DFF])
    ffn_conv_w = din("ffn_conv_w", [DEPTH, 3, 2 * DFF])
    ffn_conv_b = din("ffn_conv_b", [DEPTH, 2 * DFF])
    w_down = din("w_down", [DEPTH, DFF, DM])
    g_ffn_post = din("norm_ffn_post", [DEPTH, DM])
    cst = {n: din("c_" + n, [128, 128]) for n in CONST_NAMES}
    c_triu = din("c_triu", [128, 64])
    c_negm = din("c_negm", [128, 64])
    c_rcnt = din("c_rcnt", [128, 8, 16])

    yp = dout("yp", [2048, DM])
    ys = dout("ys", [4, TS, DM])
    opool_p = dout("opool_p", [DEPTH, 15, DPOOL])
    opool_s = dout("opool_s", [DEPTH, 4, 15, DPOOL])
    osconv_p = dout("osconv_p", [DEPTH, 3, CONV])
    osconv_s = dout("osconv_s", [DEPTH, 4, 3, CONV])
    ossd_p = dout("ossd_p", [DEPTH, 2048, 128])
    ossd_s = dout("ossd_s", [DEPTH, 4, 2048, 128])
    ofconv_p = dout("ofconv_p", [DEPTH, 2, 2 * DFF])
    ofconv_s = dout("ofconv_s", [DEPTH, 4, 2, 2 * DFF])
    hscr = nc.dram_tensor("hscr", [DEPTH, 128, 2048], F32).ap()

    st = contextlib.ExitStack()
    with st:
        def sb(name, shape, dt=F32):
            return st.enter_context(nc.sbuf_tensor(name, list(shape), dt))

        x = sb("x", [128, NKD, W])
        xn = sb("xn", [128, NKD, W], BF16)
        hP = sb("hP", [128, 2048])
        hS = sb("hS", [128, 2048])
        hpoolP = sb("hpoolP", [128, DEPTH, 8, 15])
        hsconvP = sb("hsconvP", [128, DEPTH, 20, 3])
        hfconvP = sb("hfconvP", [128, DEPTH, 88, 2])
        hpoolS = sb("hpoolS", [128, 8, 15])
        hsconvS = sb("hsconvS", [128, 20, 3])
        hfconvS = sb("hfconvS", [128, 88, 2])
        gsn = sb("gsn", [128, NKD, 5])
        pscale = sb("pscale", [128, 8, 1])
        scwb = sb("scwb", [128, 20, 5])
        fcwb = sb("fcwb", [128, 88, 4])
        dtb_bc = sb("dtb_bc", [128, 32])
        A_bc = sb("A_bc", [128, 32])
        D_bc = sb("D_bc", [128, 32])
        identf = sb("identf", [128, 128])
        identb = sb("identb", [128, 128], BF16)
        onesb = sb("onesb", [128, 128], BF16)
        cm = {n: sb("cm_" + n, [128, 128]) for n in CONST_NAMES if n != "identf"}
        triu = sb("triu", [128, 64])
        negm = sb("negm", [128, 64])
        rcnt = sb("rcnt", [128, 8, 16])
        epst = sb("epst", [128, 1])
        wslot = [sb("wslot%d" % i, [128, 4096], BF16) for i in range(2)]
        wpl = sb("wpl", [128, 8, 256], BF16)
        wdt = sb("wdt", [128, NKD, 32], BF16)
        sq = [sb("sq%d" % i, [128, W], BF16) for i in range(2)]
        rstd = sb("rstd", [128, W])
        rows = [sb("row%d" % i, [128, 600]) for i in range(5)]
        ssq = sb("ssq", [128, 5, 4])
        grs = sb("grs", [128, 5, 2])
        arena = sb("arena", [128, 21248])
        pw = [st.enter_context(nc.psum_tensor("pw%d" % i, [128, 1024], F32)) for i in range(4)]

        off = [0]

        def carve(words, dt=F32, shape=None):
            a = arena[:, off[0]:off[0] + words]
            off[0] += words
            if dt == BF16:
                a = a.bitcast(BF16)
            return a

        yssd = carve(4352, BF16).rearrange("p (c t) -> p c t", t=W)
        ypool = carve(2176, BF16).rearrange("p (c t) -> p c t", t=W)
        Bfm = carve(544, BF16).rearrange("p (c t) -> p c t", t=W)
        Cfm = carve(544, BF16).rearrange("p (c t) -> p c t", t=W)
        Btm = carve(640, BF16).rearrange("p (b g n) -> p b g n", b=5, g=2)
        dtt = carve(160).rearrange("p (b j) -> p b j", j=32)
        att = carve(160).rearrange("p (b j) -> p b j", j=32)
        acum = carve(160).rearrange("p (b j) -> p b j", j=32)
        nacum = carve(160).rearrange("p (b j) -> p b j", j=32)
        eac = carve(160).rearrange("p (b j) -> p b j", j=32)
        dtdec = carve(160).rearrange("p (b j) -> p b j", j=32)
        cdbc = carve(288).rearrange("p (c j) -> p c j", j=32)
        cbT = carve(640).rearrange("p (b g l) -> p b g l", b=5, g=2)
        xsfm = carve(1088, BF16).rearrange("p (c t) -> p c t", t=W)
        xc = carve(256, BF16)
        xcd = carve(256, BF16)
        zs_off = off[0]
        zs = carve(1280, BF16).rearrange("p (b f) -> p b f", f=512)
        Zb = carve(512)
        Db = carve(512)
        Lb = carve(256, BF16)
        Mb = carve(256, BF16)
        t1 = carve(512)
        t2 = carve(512)
        yg_off = off[0]
        yg = carve(2560, BF16).rearrange("p (b f) -> p b f", f=1024)
        yn = carve(512, BF16)
        Cz = carve(128, BF16).rearrange("p (c l) -> p c l", l=128)
        h16 = carve(256, BF16)
        stage = carve(2048)
        mixer_words = off[0]
        assert mixer_words <= 21248, mixer_words
        hff = arena[:, 0:11968].bitcast(BF16).rearrange("p (c t) -> p c t", t=W)
        stage_f = arena[:, 11968:11968 + 2048]
        wA = [arena[:, 14016 + i * 2048:14016 + (i + 1) * 2048].bitcast(BF16) for i in range(3)]

        STG_ALL = [("stage", g_, j_) for g_ in range(3) for j_ in range(5)]

        def _kl(k):
            return list(k) if isinstance(k, list) else [k]

        AM = "AM"

        def _fl(keys):
            out = []
            for k in keys:
                if isinstance(k, tuple) and len(k) > 0 and k[0] == "multi":
                    out.extend(k[1:])
                else:
                    out.append(k)
            return out

        def A(eng, fn, r=(), w=(), s=(), arena_=True):
            rr = _fl(r) + ([AM] if arena_ else [])
            return P.op(eng, fn, reads=rr, writes=_fl(w), sreads=list(s))

        def Dm(eng, fn, r=(), w=(), arena_=True):
            rr = _fl(r) + ([AM] if arena_ else [])
            return P.op(eng, fn, reads=rr, writes=_fl(w), dma=True)

        def fence():
            P.op("dve", lambda e: e.memset(epst[:, 0:1], EPS), reads=(), writes=[AM, "eps"])

        psW = Rot("psW", [pw[0], pw[1]])
        psB = Rot("psB", [pw[2][:, 0:512], pw[2][:, 512:1024], pw[3][:, 0:512], pw[3][:, 512:1024]])
        rowR = Rot("row", [r_[:, :] for r_ in rows])
        sqR = Rot("sq", [s_[:, :] for s_ in sq])
        wR = Rot("wslot", [w_[:, :] for w_ in wslot])

        class _RotF:
            def __init__(self):
                self.i = 0
                self.items = [(wslot[0][:, :], ("wslot", 0)), (wA[0], ("wA", 0)), (wslot[1][:, :], ("wslot", 1)), (wA[1], ("wA", 1)), (wA[2], ("wA", 2))]

            def next(self):
                it = self.items[self.i]
                self.i = (self.i + 1) % len(self.items)
                return it
        wF = _RotF()
        wX0 = arena[:, zs_off:zs_off + 2048].bitcast(BF16)
        wX1 = arena[:, yg_off:yg_off + 2048].bitcast(BF16)
        WX0_KEYS = ["zs", "Zb", "Db", ("wX", 0)]
        WX1_KEYS = [("yg", b_, h_) for b_ in range(5) for h_ in range(2)] + [("wX", 1)]

        class _RotO:
            def __init__(self):
                self.i = 0
                self.items = [(wslot[0][:, :], ("wslot", 0)), (wX0, ("multi",) + tuple(WX0_KEYS)), (wslot[1][:, :], ("wslot", 1)), (wX1, ("multi",) + tuple(WX1_KEYS))]

            def next(self):
                it = self.items[self.i]
                self.i = (self.i + 1) % len(self.items)
                return it
        wO = _RotO()

        for n in CONST_NAMES:
            dst = identf if n == "identf" else cm[n]
            Dm("sp", lambda e, d=dst, s=cst[n]: e.dma_start(out=d[:], in_=s[:, :]), w=[("c", n)], arena_=False)
        Dm("sp", lambda e: e.dma_start(out=triu[:], in_=c_triu[:, :]), w=["triu"], arena_=False)
        Dm("sp", lambda e: e.dma_start(out=negm[:], in_=c_negm[:, :]), w=["negm"], arena_=False)
        Dm("sp", lambda e: e.dma_start(out=rcnt[:], in_=c_rcnt[:, :, :]), w=["rcnt"], arena_=False)
        Dm("pool", lambda e: e.dma_start(out=identb[:], in_=cst["identf"][:, :]), w=["identb"], arena_=False)
        P.op("dve", lambda e: e.memset(onesb[:], 1.0), writes=["onesb"])
        fence()
        P.op("dve", lambda e: e.memset(hpoolP[:], 0.0), writes=[("hpoolP", l_, c_) for l_ in range(DEPTH) for c_ in range(8)])
        P.op("dve", lambda e: e.memset(hsconvP[:], 0.0), writes=[("hsconvP", l_, c_) for l_ in range(DEPTH) for c_ in range(20)])
        P.op("dve", lambda e: e.memset(hfconvP[:], 0.0), writes=[("hfconvP", l_, c_) for l_ in range(DEPTH) for c_ in range(88)])

        def load_tm_to_fm(src_rows, ntok, col0, key_x):
            Dm("sp", lambda e: e.dma_start(out=stage[0:ntok, :], in_=src_rows), w=STG_ALL)
            for q in range(4):
                ps, pk = psB.next()
                for i in range(4):
                    c = q * 4 + i
                    A("pe", lambda e, ps=ps, c=c, i=i: e.transpose(ps[:, i * 128:i * 128 + ntok], stage[0:ntok, c * 128:(c + 1) * 128], identf[0:ntok, 0:ntok]),
                      r=STG_ALL + [("c", "identf")], w=[pk])
                A("act", lambda e, ps=ps, q=q: e.activation(out=x[:, q * 4:(q + 1) * 4, col0:col0 + ntok],
                                                             in_=ps[:, :].rearrange("p (a b) -> p a b", b=128)[:, :, 0:ntok], func=AF.Copy),
                  r=[pk], w=[key_x])

        def store_fm_to_tm(dst_rows, ntok, col0, stg, skey):
            for q in range(4):
                ps, pk = psB.next()
                for i in range(4):
                    c = q * 4 + i
                    A("pe", lambda e, ps=ps, c=c, i=i: e.transpose(ps[0:ntok, i * 128:(i + 1) * 128], x[:, c, col0:col0 + ntok], identf[:, :]),
                      r=["x", ("c", "identf")], w=[pk])
                A("act", lambda e, ps=ps, q=q: e.activation(out=stg[0:ntok, q * 512:(q + 1) * 512], in_=ps[0:ntok, :], func=AF.Copy),
                  r=[pk], w=_kl(skey))
            Dm("sp", lambda e: e.dma_start(out=dst_rows, in_=stg[0:ntok, :]), r=_kl(skey))

        stg_i = [0]

        def rows_to_fm(srcs, C, dst, dkey):
            if not isinstance(srcs, list):
                srcs = [srcs]
            H = sum(int(a.shape[0]) for a in srcs)
            nch = C // 128
            for p0 in range(0, nch, 16):
                n = min(16, nch - p0)
                gi = stg_i[0]
                stg_i[0] = 0
                pb_ = 32 * gi
                r0 = 0
                rk = []
                for si_, a in enumerate(srcs):
                    h = int(a.shape[0])
                    Dm("sp", lambda e, a=a, r0=r0, h=h: e.dma_start(out=stage[pb_ + r0:pb_ + r0 + h, 0:n * 128], in_=a[:, p0 * 128:(p0 + n) * 128]),
                       w=[("stage", gi, si_)])
                    r0 += h
                ps, pk = psB.next()
                first = True
                for i in range(n):
                    A("pe", lambda e, ps=ps, i=i: e.transpose(ps[:, i * H:(i + 1) * H], stage[pb_:pb_ + H, i * 128:(i + 1) * 128], identf[pb_:pb_ + H, pb_:pb_ + H]),
                      r=[("stage", gi, j_) for j_ in range(5)] + [("c", "identf")], w=[pk])
                A("act", lambda e, ps=ps, p0=p0, n=n: e.activation(out=dst[:, p0:p0 + n, :], in_=ps[:, 0:n * H].rearrange("p (a b) -> p a b", b=H), func=AF.Copy),
                  r=[pk], w=[dkey])

        def fm_to_rows(src, skey, H, C, dst, stg, stkey):
            nch = C // 128
            for p0 in range(0, nch, 16):
                n = min(16, nch - p0)
                for q0 in range(0, n, 4):
                    m = min(4, n - q0)
                    ps, pk = psB.next()
                    for i in range(m):
                        A("pe", lambda e, ps=ps, i=i, c=p0 + q0 + i: e.transpose(ps[0:H, i * 128:(i + 1) * 128], src[:, c, :], identf[:, :]),
                          r=[skey(p0 + q0 + i) if callable(skey) else skey, ("c", "identf")], w=[pk])
                    A("act", lambda e, ps=ps, q0=q0, m=m: e.activation(out=stg[0:H, q0 * 128:(q0 + m) * 128], in_=ps[0:H, 0:m * 128], func=AF.Copy),
                      r=[pk], w=_kl(stkey))
                Dm("sp", lambda e, p0=p0, n=n: e.dma_start(out=dst[:, p0 * 128:(p0 + n) * 128], in_=stg[0:H, 0:n * 128]), r=_kl(stkey))

        def load_w(src, nk, ncols, rot=None):
            slot, wk = (rot or wR).next()
            v = slot[:, 0:nk * ncols].rearrange("p (k f) -> p k f", f=ncols)
            Dm("pool", lambda e: e.dma_start(out=v, in_=src.rearrange("(k p) f -> p k f", p=128)), w=[wk], arena_=(rot is not None))
            return v, wk

        def rmsnorm_stats(src_fn, skeys, nchunks, arena_):
            ps, pk = psW.next()
            for c in range(nchunks):
                s_, sk = sqR.next()
                A("act", lambda e, s_=s_, c=c: e.activation(out=s_, in_=src_fn(c), func=AF.Square), r=skeys, w=[sk], arena_=arena_)
                A("pe", lambda e, s_=s_, c=c, ps=ps: e.matmul(ps[:, 0:512], onesb[:, :], s_[:, 0:512], start=(c == 0), stop=(c == nchunks - 1)),
                  r=[sk, "onesb"], w=[pk], arena_=False)
                A("pe", lambda e, s_=s_, c=c, ps=ps: e.matmul(ps[:, 512:W], onesb[:, :], s_[:, 512:W], start=(c == 0), stop=(c == nchunks - 1)),
                  r=[sk, "onesb"], w=[pk], arena_=False)
            A("act", lambda e, ps=ps: e.activation(out=rstd[:, :], in_=ps[:, 0:W], func=AF.Ln, scale=1.0 / (nchunks * 128), bias=epst[:, 0:1]),
              r=[pk, "eps"], w=["rstd"], arena_=False)
            A("act", lambda e: e.activation(out=rstd[:, :], in_=rstd[:, :], func=AF.Exp, scale=-0.5), r=["rstd"], w=["rstd"], arena_=False)

        def mm_fm(wv, wk, c0, act_fn, akeys, nk, ps, pk, s_off, first=True, last=True, k0=0, arena_=True):
            for k in range(nk):
                A("pe", lambda e, k=k: e.matmul(ps[:, 0:512], wv[:, k, c0:c0 + 128], act_fn(k0 + k)[:, 0:512],
                                                start=(first and k == 0), stop=(last and k == nk - 1)),
                  r=[wk] + akeys, w=[pk], arena_=arena_)
                A("pe", lambda e, k=k: e.matmul(ps[:, s_off:s_off + TS], wv[:, k, c0:c0 + 128], act_fn(k0 + k)[:, 512:W],
                                                start=(first and k == 0), stop=(last and k == nk - 1)),
                  r=[wk] + akeys, w=[pk], arena_=arena_)

        def conv_fm(ps, pk, H, hP_ap, hPk, hS_ap, hSk, wcol, bcol, pkeys, out_fn, arena_):
            E, ek = rowR.next()
            L = 2 * H + W
            A("act", lambda e: e.activation(out=E[:, H:H + W + H], in_=ps[:, 0:W + H], func=AF.Copy), r=[pk], w=[ek], arena_=arena_)
            A("act", lambda e: e.activation(out=E[:, 0:H], in_=hP_ap, func=AF.Copy), r=[hPk], w=[ek], arena_=arena_)
            A("act", lambda e: e.activation(out=E[:, H + 512:2 * H + 512], in_=hS_ap, func=AF.Copy), r=[hSk], w=[ek], arena_=arena_)
            O, ok = rowR.next()
            n = W + H
            A("dve", lambda e: e.tensor_scalar(out=O[:, 0:n], in0=E[:, H:H + n], scalar1=wcol(H), scalar2=bcol, op0=ALU.mult, op1=ALU.add),
              r=[ek] + pkeys, w=[ok], arena_=arena_)
            for i in range(H):
                A("dve", lambda e, i=i: e.scalar_tensor_tensor(out=O[:, 0:n], in0=E[:, i:i + n], scalar=wcol(i), in1=O[:, 0:n], op0=ALU.mult, op1=ALU.add),
                  r=[ek] + pkeys, w=[ok], arena_=arena_)
            A("act", lambda e: e.activation(out=hP_ap, in_=E[:, 512:512 + H], func=AF.Copy), r=[ek], w=[hPk], arena_=arena_)
            A("act", lambda e: e.activation(out=hS_ap, in_=E[:, L - H:L], func=AF.Copy), r=[ek], w=[hSk], arena_=arena_)
            return O, ok

        for ti in range(NT):
            si = ti
            for l in range(NL):
                last_tile = (ti == NT - 1)
                rows_to_fm([g_mix_pre[l:l + 1, :], g_mix_post[l:l + 1, :], g_ffn_pre[l:l + 1, :], g_ffn_post[l:l + 1, :], ssd_norm[l:l + 1, :]], DM, gsn, "gsn")
                rows_to_fm([pool_scale[l:l + 1, :]], DPOOL, pscale, "pscale")
                rows_to_fm([ssd_conv_w[l, :, :], ssd_conv_b[l:l + 1, :]], CONV, scwb, "scwb")
                Dm("sp", lambda e: e.dma_start(out=dtb_bc[:, :], in_=ssd_dt_bias[l, :].partition_broadcast(128)), w=["dtb"], arena_=False)
                Dm("sp", lambda e: e.dma_start(out=A_bc[:, :], in_=ssd_a_log[l, :].partition_broadcast(128)), w=["A"], arena_=False)
                Dm("sp", lambda e: e.dma_start(out=D_bc[:, :], in_=ssd_d[l, :].partition_broadcast(128)), w=["D"], arena_=False)
                A("act", lambda e: e.activation(out=A_bc[:, :], in_=A_bc[:, :], func=AF.Exp), r=["A"], w=["A"], arena_=False)
                Dm("pool", lambda e: e.dma_start(out=wpl[:, :, :], in_=w_pool[l, :, :, :].rearrange("g (h p) d -> p (g h) d", p=128)), w=["wpl"], arena_=False)
                Dm("pool", lambda e: e.dma_start(out=wdt[:, :, :], in_=w_in[l, :, 5632:5664].rearrange("(k p) f -> p k f", p=128)), w=["wdt"], arena_=False)

                if l == 0:
                    for b in range(4):
                        load_tm_to_fm(xp[ti * TP + b * 128: ti * TP + (b + 1) * 128, :], 128, b * 128, "x")
                    load_tm_to_fm(xs[si, :, :], TS, 512, "x")

                if ti == 0:
                    A("dve", lambda e: e.memset(hP[:, :], 0.0), w=[("hP", i) for i in range(4)], arena_=False)
                else:
                    Dm("sp", lambda e: e.dma_start(out=hP[:, :], in_=hscr[l, :, :]), r=[("hscr", l)], w=[("hP", i) for i in range(4)], arena_=False)
                Dm("sp", lambda e: e.dma_start(out=stage.rearrange("p (b n) -> p b n", n=128), in_=cssd[l, si, :, :].rearrange("(b r) n -> r b n", r=128)), w=STG_ALL)
                for q in range(4):
                    ps, pk = psB.next()
                    for i in range(4):
                        b = q * 4 + i
                        A("pe", lambda e, ps=ps, i=i, b=b: e.transpose(ps[:, i * 128:(i + 1) * 128], stage[:, b * 128:(b + 1) * 128], identf[:, :]),
                          r=STG_ALL + [("c", "identf")], w=[pk])
                    A("act", lambda e, ps=ps, q=q: e.activation(out=hS[:, q * 512:(q + 1) * 512], in_=ps[:, :], func=AF.Copy), r=[pk], w=[("hS", q)], arena_=False)
                rows_to_fm([cpool[l, si, :, :]], DPOOL, hpoolS, "hpoolS")
                rows_to_fm([csconv[l, si, :, :]], CONV, hsconvS, "hsconvS")

                rmsnorm_stats(lambda c: x[:, c, :], ["x"], NKD, False)
                for c in range(NKD):
                    A("dve", lambda e, c=c: e.scalar_tensor_tensor(out=xn[:, c, :], in0=x[:, c, :], scalar=gsn[:, c, 0:1], in1=rstd[:, :], op0=ALU.mult, op1=ALU.mult),
                      r=["x", "rstd", "gsn"], w=["xn"], arena_=False)

                xn_fn = lambda k: xn[:, k, :]

                blks = [(0, 128), (128, 128), (256, 128), (384, 128), (512, TS)]
                for b, (c0, nt) in enumerate(blks):
                    ps, pk = psB.next()
                    for k in range(NKD):
                        A("pe", lambda e, k=k, ps=ps, c0=c0, nt=nt: e.matmul(ps[0:nt, 0:32], xn[:, k, c0:c0 + nt], wdt[:, k, :], start=(k == 0), stop=(k == NKD - 1)),
                          r=["xn", "wdt"], w=[pk], arena_=False)
                    if nt < 128:
                        A("dve", lambda e, b=b: e.memset(dtt[:, b, :], 0.0), w=["dtt"])
                    A("dve", lambda e, ps=ps, b=b, nt=nt: e.tensor_tensor(out=dtt[0:nt, b, :], in0=ps[0:nt, 0:32], in1=dtb_bc[0:nt, :], op=ALU.add), r=[pk, "dtb"], w=["dtt"])
                A("act", lambda e: e.activation(out=dtt[:, :, :], in_=dtt[:, :, :], func=AF.Exp), r=["dtt"], w=["dtt"])
                A("act", lambda e: e.activation(out=dtt[:, :, :], in_=dtt[:, :, :], func=AF.Ln, bias=1.0), r=["dtt"], w=["dtt"])
                A("dve", lambda e: e.tensor_scalar(out=dtt[:, 4, :], in0=dtt[:, 4, :], scalar1=cm["sel32"][:, 0:1], scalar2=None, op0=ALU.mult), r=["dtt", ("c", "sel32")], w=["dtt"])
                A("dve", lambda e: e.scalar_tensor_tensor(out=att[:, :, :], in0=dtt[:, :, :], scalar=-1.0, in1=A_bc[:, :].unsqueeze(1).broadcast_to([128, 5, 32]), op0=ALU.mult, op1=ALU.mult),
                  r=["dtt", "A"], w=["att"])
                for b, (c0, nt) in enumerate(blks):
                    tri, rest = ("tri64", "rest64") if b < 4 else ("tri32", "rest32")
                    ps, pk = psB.next()
                    A("pe", lambda e, ps=ps, b=b, tri=tri: e.matmul(ps[:, 0:32], cm[tri][:, :], att[:, b, :], start=True, stop=True), r=["att", ("c", tri)], w=[pk])
                    A("pe", lambda e, ps=ps, b=b, rest=rest: e.matmul(ps[:, 32:64], cm[rest][:, :], att[:, b, :], start=True, stop=True), r=["att", ("c", rest)], w=[pk])
                    sels = ["sel0", "sel1"] if b < 4 else ["sel32"]
                    for ci, sel in enumerate(sels):
                        A("pe", lambda e, ps=ps, b=b, sel=sel, ci=ci: e.matmul(ps[:, 64 + 32 * ci:96 + 32 * ci], cm[sel][:, :], att[:, b, :], start=True, stop=True),
                          r=["att", ("c", sel)], w=[pk])
                    A("act", lambda e, ps=ps, b=b: e.activation(out=acum[:, b, :], in_=ps[:, 0:32], func=AF.Copy), r=[pk], w=["acum"])
                    A("act", lambda e, ps=ps, b=b: e.activation(out=nacum[:, b, :], in_=ps[:, 0:32], func=AF.Copy, scale=-1.0), r=[pk], w=["nacum"])
                    A("act", lambda e, ps=ps, b=b: e.activation(out=eac[:, b, :], in_=ps[:, 0:32], func=AF.Exp), r=[pk], w=["eac"])
                    A("act", lambda e, ps=ps, b=b: e.activation(out=dtdec[:, b, :], in_=ps[:, 32:64], func=AF.Exp), r=[pk], w=["dtdec"])
                    for ci in range(len(sels)):
                        A("act", lambda e, ps=ps, b=b, ci=ci: e.activation(out=cdbc[:, 2 * b + ci, :], in_=ps[:, 64 + 32 * ci:96 + 32 * ci], func=AF.Exp), r=[pk], w=["cdbc"])
                A("dve", lambda e: e.tensor_tensor(out=dtdec[:, :, :], in0=dtdec[:, :, :], in1=dtt[:, :, :], op=ALU.mult), r=["dtdec", "dtt"], w=["dtdec"])

                wv, wk = load_w(w_in[l, :, 5120:5376], NKD, 256)
                wv2, wk2 = load_w(w_in[l, :, 5376:5632], NKD, 256)
                for which, (wv_, wk_, dst, dkey) in enumerate(((wv, wk, Bfm, "Bfm"), (wv2, wk2, Cfm, "Cfm"))):
                    for g in range(2):
                        cidx = 16 + which * 2 + g
                        ps, pk = psW.next()
                        mm_fm(wv_, wk_, g * 128, xn_fn, ["xn"], NKD, ps, pk, 512 + 3, arena_=False)
                        O, ok = conv_fm(ps, pk, 3, hsconvP[:, l, cidx, :], ("hsconvP", l, cidx), hsconvS[:, cidx, :], "hsconvS",
                                        lambda i, cidx=cidx: scwb[:, cidx, i:i + 1], scwb[:, cidx, 4:5], ["scwb"], None, False)
                        A("act", lambda e, O=O, dst=dst, g=g: e.activation(out=dst[:, g, 0:512], in_=O[:, 0:512], func=AF.Silu), r=[ok], w=[dkey])
                        A("act", lambda e, O=O, dst=dst, g=g: e.activation(out=dst[:, g, 512:W], in_=O[:, 515:515 + TS], func=AF.Silu), r=[ok], w=[dkey])
                for b, (c0, nt) in enumerate(blks):
                    ps, pk = psB.next()
                    pb = ps[:, 0:128].bitcast(BF16)
                    for g in range(2):
                        A("pe", lambda e, pb=pb, g=g, c0=c0, nt=nt: e.transpose(pb[0:nt, g * 128:(g + 1) * 128], Bfm[:, g, c0:c0 + nt], identb[:, :]),
                          r=["Bfm", "identb"], w=[pk])
                    A("act", lambda e, pb=pb, b=b, nt=nt: e.activation(out=Btm[0:nt, b, :, :], in_=pb[0:nt, 0:256].rearrange("p (g n) -> p g n", n=128), func=AF.Copy),
                      r=[pk], w=["Btm"])
                    ps2, pk2 = psB.next()
                    for g in range(2):
                        A("pe", lambda e, ps2=ps2, g=g, c0=c0, nt=nt: e.matmul(ps2[0:nt, g * 128:g * 128 + nt], Bfm[:, g, c0:c0 + nt], Cfm[:, g, c0:c0 + nt], start=True, stop=True),
                          r=["Bfm", "Cfm"], w=[pk2])
                    if nt == 128:
                        A("act", lambda e, ps2=ps2, b=b: e.activation(out=cbT[0:64, b, :, :], in_=ps2[0:64, 0:256].rearrange("p (g l) -> p g l", l=128)[:, :, 0:64], func=AF.Copy), r=[pk2], w=["cbT"])
                        A("act", lambda e, ps2=ps2, b=b: e.activation(out=cbT[64:128, b, :, :], in_=ps2[64:128, 0:256].rearrange("p (g l) -> p g l", l=128)[:, :, 64:128], func=AF.Copy), r=[pk2], w=["cbT"])
                    else:
                        A("dve", lambda e, b=b: e.memset(cbT[:, b, :, :], 0.0), w=["cbT"])
                        A("act", lambda e, ps2=ps2, b=b: e.activation(out=cbT[0:TS, b, :, 0:TS], in_=ps2[0:TS, 0:256].rearrange("p (g l) -> p g l", l=128)[:, :, 0:TS], func=AF.Copy), r=[pk2], w=["cbT"])

                for g in range(4):
                    wv, wk = load_w(w_in[l, :, g * 256:(g + 1) * 256], NKD, 256)
                    dts = []
                    for h2 in range(2):
                        c = g * 2 + h2
                        ps, pk = psW.next()
                        mm_fm(wv, wk, h2 * 128, xn_fn, ["xn"], NKD, ps, pk, 512 + 15, arena_=False)
                        E, ek = rowR.next()
                        H = 15
                        A("act", lambda e, E=E, ps=ps: e.activation(out=E[:, H:H + W + H], in_=ps[:, 0:W + H], func=AF.Copy), r=[pk], w=[ek], arena_=False)
                        A("act", lambda e, E=E, c=c: e.activation(out=E[:, 0:H], in_=hpoolP[:, l, c, :], func=AF.Copy), r=[("hpoolP", l, c)], w=[ek], arena_=False)
                        A("act", lambda e, E=E, c=c: e.activation(out=E[:, H + 512:2 * H + 512], in_=hpoolS[:, c, :], func=AF.Copy), r=["hpoolS"], w=[ek], arena_=False)
                        A("act", lambda e, E=E, c=c: e.activation(out=hpoolP[:, l, c, :], in_=E[:, 512:512 + H], func=AF.Copy), r=[ek], w=[("hpoolP", l, c)], arena_=False)
                        A("act", lambda e, E=E, c=c: e.activation(out=hpoolS[:, c, :], in_=E[:, 2 * H + W - H:2 * H + W], func=AF.Copy), r=[ek], w=["hpoolS"], arena_=False)
                        n = W + H
                        cur, ck = E, ek
                        wdw = 1
                        for step in range(g + 1):
                            nxt, nk_ = rowR.next()
                            A("dve", lambda e, cur=cur, nxt=nxt, wdw=wdw: e.tensor_tensor(out=nxt[:, H:H + n], in0=cur[:, H:H + n], in1=cur[:, H - wdw:H - wdw + n], op=ALU.add),
                              r=[ck], w=[nk_], arena_=False)
                            if step < g:
                                A("dve", lambda e, cur=cur, nxt=nxt, wdw=wdw: e.tensor_tensor(out=nxt[:, wdw:H], in0=cur[:, wdw:H], in1=cur[:, 0:H - wdw], op=ALU.add),
                                  r=[ck], w=[nk_], arena_=False)
                            cur, ck = nxt, nk_
                            wdw *= 2
                        wsz = 2 ** (g + 1)
                        dk = ck
                        if ti == 0:
                            A("dve", lambda e, cur=cur, c=c: e.tensor_tensor(out=cur[:, H:H + 16], in0=cur[:, H:H + 16], in1=rcnt[:, c, :], op=ALU.mult), r=[ck, "rcnt"], w=[ck], arena_=False)
                        A("dve", lambda e, cur=cur, E=E, wsz=wsz: e.scalar_tensor_tensor(out=cur[:, H:H + n], in0=cur[:, H:H + n], scalar=1.0 / wsz, in1=E[:, H:H + n], op0=ALU.mult, op1=ALU.subtract),
                          r=[ck, ek], w=[ck], arena_=False)
                        s_, sk = sqR.next()
                        A("act", lambda e, s_=s_, cur=cur: e.activation(out=s_[:, 0:512], in_=cur[:, H:H + 512], func=AF.Copy), r=[dk], w=[sk], arena_=False)
                        A("act", lambda e, s_=s_, cur=cur: e.activation(out=s_[:, 512:W], in_=cur[:, 512 + 2 * H:W + 2 * H], func=AF.Copy), r=[dk], w=[sk], arena_=False)
                        dts.append((s_, sk))
                    for h2 in range(2):
                        c = g * 2 + h2
                        ps, pk = psW.next()
                        for kk in range(2):
                            s_, sk = dts[kk]
                            A("pe", lambda e, ps=ps, s_=s_, kk=kk, h2=h2: e.matmul(ps[:, 0:512], wpl[:, g * 2 + kk, h2 * 128:(h2 + 1) * 128], s_[:, 0:512], start=(kk == 0), stop=(kk == 1)),
                              r=[sk, "wpl"], w=[pk], arena_=False)
                            A("pe", lambda e, ps=ps, s_=s_, kk=kk, h2=h2: e.matmul(ps[:, 512:W], wpl[:, g * 2 + kk, h2 * 128:(h2 + 1) * 128], s_[:, 512:W], start=(kk == 0), stop=(kk == 1)),
                              r=[sk, "wpl"], w=[pk], arena_=False)
                        A("act", lambda e, ps=ps, c=c: e.activation(out=ypool[:, c, :], in_=ps[:, 0:W], func=AF.Identity, scale=pscale[:, c, 0:1]), r=[pk], s=["pscale"], w=["ypool"])

                A("dve", lambda e: e.memset(ssq[:, :, :], 0.0), w=["ssq"], arena_=False)
                for hb in range(4):
                    g = hb // 2
                    for half in range(2):
                        wv, wk = load_w(w_in[l, :, 3072 + hb * 512 + half * 256: 3072 + hb * 512 + (half + 1) * 256], NKD, 256)
                        for h2 in range(2):
                            cc = half * 2 + h2
                            cidx = hb * 4 + cc
                            ps, pk = psW.next()
                            mm_fm(wv, wk, h2 * 128, xn_fn, ["xn"], NKD, ps, pk, 512 + 3, arena_=False)
                            O, ok = conv_fm(ps, pk, 3, hsconvP[:, l, cidx, :], ("hsconvP", l, cidx), hsconvS[:, cidx, :], "hsconvS",
                                            lambda i, cidx=cidx: scwb[:, cidx, i:i + 1], scwb[:, cidx, 4:5], ["scwb"], None, False)
                            A("act", lambda e, O=O, cc=cc: e.activation(out=xsfm[:, cc, 0:512], in_=O[:, 0:512], func=AF.Silu), r=[ok], w=["xsfm"])
                            A("act", lambda e, O=O, cc=cc: e.activation(out=xsfm[:, cc, 512:W], in_=O[:, 515:515 + TS], func=AF.Silu), r=[ok], w=["xsfm"])
                    for half in range(2):
                        wv, wk = load_w(w_in[l, :, 1024 + hb * 512 + half * 256: 1024 + hb * 512 + (half + 1) * 256], NKD, 256)
                        for b, (c0, nt) in enumerate(blks):
                            ps, pk = psB.next()
                            for k in range(NKD):
                                A("pe", lambda e, k=k, ps=ps, c0=c0, nt=nt, wv=wv: e.matmul(ps[0:nt, 0:256], xn[:, k, c0:c0 + nt], wv[:, k, :], start=(k == 0), stop=(k == NKD - 1)),
                                  r=["xn", wk], w=[pk], arena_=False)
                            A("act", lambda e, ps=ps, b=b, nt=nt, half=half: e.activation(out=zs[0:nt, b, half * 256:(half + 1) * 256], in_=ps[0:nt, 0:256], func=AF.Silu), r=[pk], w=["zs"])
                    for b, (c0, nt) in enumerate(blks):
                        isS = (b == 4)
                        q = TS if isS else 64
                        nch = 1 if isS else 2
                        hst, hk = (hS, "hS") if isS else (hP, "hP")
                        j0 = hb * 8
                        v3 = lambda ap: ap.rearrange("p (j d) -> p j d", d=64)
                        ps, pk = psB.next()
                        pb = ps[:, 0:256].bitcast(BF16)
                        for cc in range(4):
                            A("pe", lambda e, pb=pb, cc=cc, c0=c0, nt=nt: e.transpose(pb[0:nt, cc * 128:(cc + 1) * 128], xsfm[:, cc, c0:c0 + nt], identb[:, :]), r=["xsfm", "identb"], w=[pk])
                        A("dve", lambda e: e.tensor_tensor(out=v3(Zb), in0=att[:, b, j0:j0 + 8].unsqueeze(2).broadcast_to([128, 8, 64]),
                                                           in1=triu[:, :].unsqueeze(1).broadcast_to([128, 8, 64]), op=ALU.mult), r=["att", "triu"], w=["Zb"])
                        psr, pkr = psB.next()
                        blkm = "blk32" if isS else "blk64"
                        A("pe", lambda e: e.matmul(psr[:, :], cm[blkm][:, :], Zb, start=True, stop=True), r=["Zb", ("c", blkm)], w=[pkr])
                        A("dve", lambda e: e.tensor_tensor(out=v3(xcd[0:nt, :]), in0=v3(pb[0:nt, :]), in1=dtdec[0:nt, b, j0:j0 + 8].unsqueeze(2).broadcast_to([nt, 8, 64]), op=ALU.mult), r=[pk, "dtdec"], w=["xcd"])
                        A("dve", lambda e: e.tensor_tensor(out=v3(xc[0:nt, :]), in0=v3(pb[0:nt, :]), in1=dtt[0:nt, b, j0:j0 + 8].unsqueeze(2).broadcast_to([nt, 8, 64]), op=ALU.mult), r=[pk, "dtt"], w=["xc"])
                        A("dve", lambda e: e.tensor_tensor(out=v3(t2[0:nt, :]), in0=v3(pb[0:nt, :]), in1=D_bc[0:nt, j0:j0 + 8].unsqueeze(2).broadcast_to([nt, 8, 64]), op=ALU.mult), r=[pk, "D"], w=["t2"])
                        A("dve", lambda e: e.memset(Cz[:, :, :], 0.0), w=["Cz"])
                        for ch in range(nch):
                            A("act", lambda e, ch=ch: e.activation(out=Cz[:, ch, ch * 64:ch * 64 + q], in_=Cfm[:, g, c0 + ch * 64:c0 + ch * 64 + q], func=AF.Copy), r=["Cfm"], w=["Cz"])
                        pso, pko = psB.next()

                        def state_step(ch):
                            p0 = ch * 64
                            A("act", lambda e: e.activation(out=h16[:, :], in_=hst[:, hb * 512:(hb + 1) * 512], func=AF.Copy), r=[(hk, hb)], w=["h16"], arena_=True)
                            A("pe", lambda e: e.matmul(pso[:, :], Cz[:, ch, :], h16[:, :], start=(ch == 0), stop=(ch == nch - 1)), r=["Cz", "h16"], w=[pko])
                            pss, pks = psB.next()
                            A("pe", lambda e: e.matmul(pss[:, :], Btm[p0:p0 + q, b, g, :], xcd[p0:p0 + q, :], start=True, stop=True), r=["Btm", "xcd"], w=[pks])
                            cidx = 2 * b + ch
                            A("dve", lambda e: e.tensor_tensor(out=v3(hst[:, hb * 512:(hb + 1) * 512]), in0=v3(hst[:, hb * 512:(hb + 1) * 512]),
                                                               in1=cdbc[:, cidx, j0:j0 + 8].unsqueeze(2).broadcast_to([128, 8, 64]), op=ALU.mult), r=["cdbc", "h16"], w=[(hk, hb)], arena_=True)
                            A("dve", lambda e: e.tensor_tensor(out=hst[:, hb * 512:(hb + 1) * 512], in0=hst[:, hb * 512:(hb + 1) * 512], in1=pss[:, :], op=ALU.add), r=[pks], w=[(hk, hb)])

                        A("dve", lambda e: e.tensor_tensor(out=v3(Db), in0=v3(psr[:, :]), in1=nacum[:, b, j0:j0 + 8].unsqueeze(2).broadcast_to([128, 8, 64]), op=ALU.add), r=[pkr, "nacum"], w=["Db"])
                        A("dve", lambda e: e.tensor_tensor(out=v3(Db), in0=v3(Db), in1=negm[:, :].unsqueeze(1).broadcast_to([128, 8, 64]), op=ALU.min), r=["Db", "negm"], w=["Db"])
                        A("act", lambda e: e.activation(out=Lb, in_=Db, func=AF.Exp), r=["Db"], w=["Lb"])
                        state_step(0)
                        A("dve", lambda e: e.tensor_tensor(out=v3(Mb), in0=v3(Lb), in1=cbT[:, b, g, :].unsqueeze(1).broadcast_to([128, 8, 64]), op=ALU.mult), r=["Lb", "cbT"], w=["Mb"])
                        psd, pkd = psB.next()
                        for jj in range(8):
                            for ch in range(nch):
                                p0 = ch * 64
                                A("pe", lambda e, jj=jj, p0=p0: e.matmul(psd[p0:p0 + q, jj * 64:(jj + 1) * 64], Mb[p0:p0 + q, jj * 64:jj * 64 + q], xc[p0:p0 + q, jj * 64:(jj + 1) * 64], start=True, stop=True),
                                  r=["Mb", "xc"], w=[pkd])
                        if nch == 2:
                            state_step(1)
                        A("dve", lambda e, pso=pso, nt=nt, b=b: e.tensor_tensor(out=t1[0:nt, :].rearrange("p (j d) -> p j d", d=64), in0=pso[0:nt, :].rearrange("p (j d) -> p j d", d=64),
                                                                          in1=eac[0:nt, b, j0:j0 + 8].unsqueeze(2).broadcast_to([nt, 8, 64]), op=ALU.mult), r=[pko, "eac"], w=["t1"])
                        A("dve", lambda e, psd=psd, nt=nt: e.tensor_tensor(out=t1[0:nt, :], in0=psd[0:nt, :], in1=t1[0:nt, :], op=ALU.add), r=[pkd, "t1"], w=["t1"])
                        A("dve", lambda e, nt=nt: e.tensor_tensor(out=t1[0:nt, :], in0=t1[0:nt, :], in1=t2[0:nt, :], op=ALU.add), r=["t1", "t2"], w=["t1"])
                        hh = hb % 2
                        A("dve", lambda e, nt=nt, b=b, hh=hh: e.tensor_tensor(out=yg[0:nt, b, hh * 512:(hh + 1) * 512], in0=t1[0:nt, :], in1=zs[0:nt, b, :], op=ALU.mult), r=["t1", "zs"], w=[("yg", b, hh)])
                        A("act", lambda e, nt=nt, b=b, hh=hh: e.activation(out=t2[0:nt, :], in_=yg[0:nt, b, hh * 512:(hh + 1) * 512], func=AF.Square, accum_out=ssq[0:nt, b, hb:hb + 1]), r=[("yg", b, hh)], w=["t2", "ssq"])
                        if hh == 1:
                            A("dve", lambda e, nt=nt, b=b: e.tensor_tensor(out=grs[0:nt, b, g:g + 1], in0=ssq[0:nt, b, hb - 1:hb], in1=ssq[0:nt, b, hb:hb + 1], op=ALU.add), r=["ssq"], w=["grs"])
                            A("act", lambda e, nt=nt, b=b: e.activation(out=grs[0:nt, b, g:g + 1], in_=grs[0:nt, b, g:g + 1], func=AF.Ln, scale=1.0 / 1024, bias=epst[0:nt, 0:1]), r=["grs", "eps"], w=["grs"])
                            A("act", lambda e, nt=nt, b=b: e.activation(out=grs[0:nt, b, g:g + 1], in_=grs[0:nt, b, g:g + 1], func=AF.Exp, scale=-0.5), r=["grs"], w=["grs"])
                            A("dve", lambda e, nt=nt, b=b: e.tensor_scalar(out=yn[0:nt, :], in0=yg[0:nt, b, :], scalar1=grs[0:nt, b, g:g + 1], scalar2=None, op0=ALU.mult),
                              r=[("yg", b, 0), ("yg", b, 1)], s=["grs"], w=["yn"])
                            for half in range(2):
                                pst, pkt = psB.next()
                                pbt = pst[:, 0:256].bitcast(BF16)
                                for i in range(4):
                                    fc = half * 4 + i
                                    A("pe", lambda e, pbt=pbt, i=i, fc=fc, nt=nt: e.transpose(pbt[:, i * 128:i * 128 + nt], yn[0:nt, fc * 128:(fc + 1) * 128], identb[0:nt, 0:nt]), r=["yn", "identb"], w=[pkt])
                                A("dve", lambda e, pbt=pbt, half=half, c0=c0, nt=nt: e.tensor_tensor(out=yssd[:, g * 8 + half * 4:g * 8 + half * 4 + 4, c0:c0 + nt],
                                                                                               in0=pbt[:, :].rearrange("p (a t) -> p a t", t=128)[:, :, 0:nt],
                                                                                               in1=gsn[:, g * 8 + half * 4:g * 8 + half * 4 + 4, 4:5].broadcast_to([128, 4, nt]), op=ALU.mult),
                                  r=[pkt, "gsn"], w=["yssd"])

                def store_state(hst, hk, dst):
                    for qd in range(4):
                        ps, pk = psB.next()
                        for i in range(4):
                            bb = qd * 4 + i
                            A("pe", lambda e, ps=ps, i=i, bb=bb: e.transpose(ps[:, i * 128:(i + 1) * 128], hst[:, bb * 128:(bb + 1) * 128], identf[:, :]),
                              r=[(hk, bb // 4), ("c", "identf")], w=[pk])
                        A("act", lambda e, ps=ps, qd=qd: e.activation(out=stage[:, qd * 512:(qd + 1) * 512], in_=ps[:, :], func=AF.Copy), r=[pk], w=STG_ALL)
                    Dm("sp", lambda e: e.dma_start(out=dst.rearrange("(b r) n -> r b n", r=128), in_=stage.rearrange("p (b n) -> p b n", n=128)), r=STG_ALL)

                store_state(hS, "hS", ossd_s[l, si, :, :])
                fm_to_rows(hpoolS, "hpoolS", 15, DPOOL, opool_s[l, si, :, :], stage, STG_ALL)
                fm_to_rows(hsconvS, "hsconvS", 3, CONV, osconv_s[l, si, :, :], stage, STG_ALL)
                if last_tile:
                    store_state(hP, "hP", ossd_p[l, :, :])
                    fm_to_rows(hpoolP[:, l, :, :], lambda c: ("hpoolP", l, c), 15, DPOOL, opool_p[l, :, :], stage, STG_ALL)
                    fm_to_rows(hsconvP[:, l, :, :], lambda c: ("hsconvP", l, c), 3, CONV, osconv_p[l, :, :], stage, STG_ALL)
                elif NT > 1:
                    Dm("sp", lambda e: e.dma_start(out=hscr[l, :, :], in_=hP[:, :]), r=[("hP", i) for i in range(4)], w=[("hscr", l)], arena_=False)

                rows_to_fm([ffn_conv_w[l, :, :], ffn_conv_b[l:l + 1, :]], 2 * DFF, fcwb, "fcwb")
                rows_to_fm([cfconv[l, si, :, :]], 2 * DFF, hfconvS, "hfconvS")

                def mixin(k):
                    return ypool[:, k, :] if k < 8 else yssd[:, k - 8, :]
                for cb in range(8):
                    wv, wk = load_w(w_out[l, :, cb * 256:(cb + 1) * 256].rearrange("(h k) f -> h k f", h=2)[0], 12, 256, wO)
                    wvb, wkb = load_w(w_out[l, :, cb * 256:(cb + 1) * 256].rearrange("(h k) f -> h k f", h=2)[1], 12, 256, wO)
                    for h2 in range(2):
                        c = cb * 2 + h2
                        ps, pk = psW.next()
                        mm_fm(wv, wk, h2 * 128, mixin, ["ypool", "yssd"], 12, ps, pk, 512, first=True, last=False, k0=0)
                        mm_fm(wvb, wkb, h2 * 128, mixin, ["ypool", "yssd"], 12, ps, pk, 512, first=False, last=True, k0=12)
                        A("act", lambda e, ps=ps, c=c: e.activation(out=xn[:, c, :], in_=ps[:, 0:W], func=AF.Copy), r=[pk], w=["xn"], arena_=False)
                rmsnorm_stats(lambda c: xn[:, c, :], ["xn"], NKD, False)
                for c in range(NKD):
                    t_, tk = rowR.next()
                    A("dve", lambda e, c=c, t_=t_: e.scalar_tensor_tensor(out=t_[:, 0:W], in0=xn[:, c, :], scalar=gsn[:, c, 1:2], in1=rstd[:, :], op0=ALU.mult, op1=ALU.mult),
                      r=["xn", "rstd", "gsn"], w=[tk], arena_=False)
                    A("dve", lambda e, c=c, t_=t_: e.tensor_tensor(out=x[:, c, :], in0=x[:, c, :], in1=t_[:, 0:W], op=ALU.add), r=[tk], w=["x"], arena_=False)

                fence()
                rmsnorm_stats(lambda c: x[:, c, :], ["x"], NKD, False)
                for c in range(NKD):
                    A("dve", lambda e, c=c: e.scalar_tensor_tensor(out=xn[:, c, :], in0=x[:, c, :], scalar=gsn[:, c, 2:3], in1=rstd[:, :], op0=ALU.mult, op1=ALU.mult),
                      r=["x", "rstd", "gsn"], w=["xn"], arena_=False)
                for cb in range(22):
                    wg, wgk = load_w(w_up[l, :, cb * 256:(cb + 1) * 256], NKD, 256, wF)
                    wvv, wvk = load_w(w_up[l, :, DFF + cb * 256:DFF + (cb + 1) * 256], NKD, 256, wF)
                    for h2 in range(2):
                        c = cb * 2 + h2
                        outs = []
                        for (wv_, wk_, cidx) in ((wg, wgk, c), (wvv, wvk, 44 + c)):
                            ps, pk = psW.next()
                            mm_fm(wv_, wk_, h2 * 128, xn_fn, ["xn"], NKD, ps, pk, 512 + 2, arena_=True)
                            O, ok = conv_fm(ps, pk, 2, hfconvP[:, l, cidx, :], ("hfconvP", l, cidx), hfconvS[:, cidx, :], "hfconvS",
                                            lambda i, cidx=cidx: fcwb[:, cidx, i:i + 1], fcwb[:, cidx, 3:4], ["fcwb"], None, False)
                            outs.append((O, ok))
                        (Og, ogk), (Ov, ovk) = outs
                        A("act", lambda e, Og=Og: e.activation(out=Og[:, 0:W + 2], in_=Og[:, 0:W + 2], func=AF.Silu), r=[ogk], w=[ogk], arena_=False)
                        A("dve", lambda e, Og=Og, Ov=Ov, c=c: e.tensor_tensor(out=hff[:, c, 0:512], in0=Og[:, 0:512], in1=Ov[:, 0:512], op=ALU.mult), r=[ogk, ovk], w=["hff"])
                        A("dve", lambda e, Og=Og, Ov=Ov, c=c: e.tensor_tensor(out=hff[:, c, 512:W], in0=Og[:, 514:514 + TS], in1=Ov[:, 514:514 + TS], op=ALU.mult), r=[ogk, ovk], w=["hff"])
                fm_to_rows(hfconvS, "hfconvS", 2, 2 * DFF, ofconv_s[l, si, :, :], stage_f, "stage_f")
                if last_tile:
                    fm_to_rows(hfconvP[:, l, :, :], lambda c: ("hfconvP", l, c), 2, 2 * DFF, ofconv_p[l, :, :], stage_f, "stage_f")
                hff_fn = lambda k: hff[:, k, :]
                for c in range(NKD):
                    wa, wak = load_w(w_down[l, 0:2816, c * 128:(c + 1) * 128], 22, 128, wF)
                    wb_, wbk = load_w(w_down[l, 2816:5632, c * 128:(c + 1) * 128], 22, 128, wF)
                    ps, pk = psW.next()
                    mm_fm(wa, wak, 0, hff_fn, ["hff"], 22, ps, pk, 512, first=True, last=False, k0=0)
                    mm_fm(wb_, wbk, 0, hff_fn, ["hff"], 22, ps, pk, 512, first=False, last=True, k0=22)
                    A("act", lambda e, ps=ps, c=c: e.activation(out=xn[:, c, :], in_=ps[:, 0:W], func=AF.Copy), r=[pk], w=["xn"], arena_=False)
                rmsnorm_stats(lambda c: xn[:, c, :], ["xn"], NKD, False)
                for c in range(NKD):
                    t_, tk = rowR.next()
                    A("dve", lambda e, c=c, t_=t_: e.scalar_tensor_tensor(out=t_[:, 0:W], in0=xn[:, c, :], scalar=gsn[:, c, 3:4], in1=rstd[:, :], op0=ALU.mult, op1=ALU.mult),
                      r=["xn", "rstd", "gsn"], w=[tk], arena_=False)
                    A("dve", lambda e, c=c, t_=t_: e.tensor_tensor(out=x[:, c, :], in0=x[:, c, :], in1=t_[:, 0:W], op=ALU.add), r=[tk], w=["x"], arena_=False)
                fence()
                if l == NL - 1:
                    for b in range(4):
                        store_fm_to_tm(yp[ti * TP + b * 128: ti * TP + (b + 1) * 128, :], 128, b * 128, stage, STG_ALL)
                    store_fm_to_tm(ys[si, :, :], TS, 512, stage, STG_ALL)

        P.emit(nc, st)
    return nc


def make_in_maps(inputs, n_cores=8):
    c = make_consts()
    rc = np.ones((128, 8, 16), np.float32)
    for ch in range(8):
        wdw = 2 ** (ch // 2 + 1)
        pos = np.arange(16)
        rc[:, ch, :] = wdw / np.minimum(pos + 1, wdw)
    maps = []
    f = lambda a: np.ascontiguousarray(np.asarray(a, dtype=np.float32))
    shared = {k: f(inputs[k]) for k in ("norm_mix_pre", "w_in", "w_pool", "pool_scale", "ssd_conv_w", "ssd_conv_b", "ssd_dt_bias",
                                        "ssd_a_log", "ssd_d", "ssd_norm", "w_out", "norm_mix_post", "norm_ffn_pre", "w_up",
                                        "ffn_conv_w", "ffn_conv_b", "w_down", "norm_ffn_post")}
    for n in CONST_NAMES:
        shared["c_" + n] = c[n]
    shared["c_triu"] = c["triu"]
    shared["c_negm"] = c["negm"]
    shared["c_rcnt"] = rc
    for i in range(n_cores):
        m = dict(shared)
        m["xp"] = f(inputs["x_prompt"][i])
        m["xs"] = f(inputs["x_sample"][4 * i:4 * i + 4])
        m["cpool"] = f(inputs["cache_pool"][:, 4 * i:4 * i + 4])
        m["csconv"] = f(inputs["state_ssd_conv"][:, 4 * i:4 * i + 4])
        m["cssd"] = f(np.asarray(inputs["state_ssd"])[:, 4 * i:4 * i + 4].reshape(DEPTH, 4, 2048, 128))
        m["cfconv"] = f(inputs["state_ffn_conv"][:, 4 * i:4 * i + 4])
        maps.append(m)
    return maps


def gather(results):
    n = len(results)
    cat = lambda k, ax: np.concatenate([np.asarray(r[k]) for r in results], axis=ax)
    y_p = np.stack([np.asarray(r["yp"]) for r in results], 0)
    y_s = cat("ys", 0)
    pool_p = np.stack([np.asarray(r["opool_p"]) for r in results], 1)
    pool_s = cat("opool_s", 1)
    sconv_p = np.stack([np.asarray(r["osconv_p"]) for r in results], 1)
    sconv_s = cat("osconv_s", 1)
    ssd_p = np.stack([np.asarray(r["ossd_p"]) for r in results], 1).reshape(DEPTH, n, 32, 64, 128)
    ssd_s = cat("ossd_s", 1).reshape(DEPTH, 4 * n, 32, 64, 128)
    fconv_p = np.stack([np.asarray(r["ofconv_p"]) for r in results], 1)
    fconv_s = cat("ofconv_s", 1)
    return tuple(np.ascontiguousarray(a, dtype=np.float32) for a in
                 (y_p, y_s, pool_p, pool_s, sconv_p, sconv_s, ssd_p, ssd_s, fconv_p, fconv_s))


def kernel(**inputs):
    nc = build_nc()
    maps = make_in_maps(inputs, 8)
    res = run_bass_kernel_spmd(nc, maps, core_ids=list(range(8)))
    return gather(res.results)
```

```python
import contextlib
import numpy as np
import concourse.bass as bass
import concourse.mybir as mybir
from concourse.bass_utils import run_bass_kernel_spmd

F32 = mybir.dt.float32
BF16 = mybir.dt.bfloat16
AF = mybir.ActivationFunctionType
ALU = mybir.AluOpType

DEPTH = 4
DM = 2048
DPOOL = 1024
DSSD = 2048
CONV = 2560
DFF = 5632
DIN = 5664
EPS = 1e-6
TP = 512
TS = 32
W = TP + TS
NKD = DM // 128
COMPUTE = ("pe", "act", "dve", "pool")


class Op:
    __slots__ = ("eng", "fn", "deps", "inc", "val", "dma", "sem_i")

    def __init__(self, eng, fn, dma):
        self.eng = eng
        self.fn = fn
        self.deps = []
        self.inc = False
        self.val = 0
        self.dma = dma
        self.sem_i = 0


class _Rec:
    def __init__(self):
        self.call = None

    def __getattr__(self, name):
        def f(*a, **k):
            self.call = (name, a, k)
            return self
        return f


class Prog:
    NDMA_SEM = 8

    def __init__(self):
        self.ops = {e: [] for e in ("pe", "act", "dve", "pool", "sp")}
        self.last_w = {}
        self.readers = {}

    def op(self, eng, fn, reads=(), writes=(), dma=False, sreads=()):
        rec = _Rec()
        fn(rec)
        o = Op(eng, rec.call, dma)
        deps = {}

        def add(d, force=False):
            if d is None or d is o:
                return
            if d.eng == o.eng and not d.dma and not o.dma and not force:
                return
            deps[id(d)] = d

        for k in reads:
            add(self.last_w.get(k), force=(eng != "pe"))
        for k in sreads:
            add(self.last_w.get(k), force=True)
        for k in writes:
            add(self.last_w.get(k))
            for r in self.readers.get(k, ()):
                add(r)
        for k in list(reads) + list(sreads):
            self.readers.setdefault(k, []).append(o)
        for k in writes:
            self.last_w[k] = o
            self.readers[k] = []
        o.deps = list(deps.values())
        self.ops[eng].append(o)
        return o

    def emit(self, nc, stack):
        for e, lst in self.ops.items():
            for o in lst:
                for d in o.deps:
                    d.inc = True
        sems = {e: stack.enter_context(nc.semaphore("s_" + e)) for e in COMPUTE}
        dsems = {e: [stack.enter_context(nc.semaphore("d_%s%d" % (e, i))) for i in range(self.NDMA_SEM)]
                 for e in ("sp", "pool", "act")}
        for e, lst in self.ops.items():
            cnt = 0
            dcnt = 0
            for o in lst:
                if o.dma:
                    o.sem_i = dcnt % self.NDMA_SEM
                    o.val = 16 * (dcnt // self.NDMA_SEM + 1)
                    dcnt += 1
                elif o.inc:
                    cnt += 1
                    o.val = cnt
        engs = {"pe": "tensor", "act": "scalar", "dve": "vector", "pool": "gpsimd", "sp": "sync"}
        block = stack.enter_context(nc.Block())

        def run(e, handle):
            waited = {}
            pend = [None] * self.NDMA_SEM

            def wait_dma(p):
                key = ("d", p.eng, p.sem_i)
                if waited.get(key, 0) < p.val:
                    handle.wait_ge(dsems[p.eng][p.sem_i], p.val)
                    waited[key] = p.val

            for o in self.ops[e]:
                if o.dma:
                    if pend[o.sem_i] is not None:
                        wait_dma(pend[o.sem_i])
                    pend[o.sem_i] = o
                need = {}
                for d in o.deps:
                    key = ("d", d.eng, d.sem_i) if d.dma else ("c", d.eng)
                    if key not in need or need[key].val < d.val:
                        need[key] = d
                for key, d in need.items():
                    if d.dma:
                        wait_dma(d)
                    elif waited.get(key, 0) < d.val:
                        handle.wait_ge(sems[d.eng], d.val)
                        waited[key] = d.val
                name_, a_, k_ = o.fn
                ins = getattr(handle, name_)(*a_, **k_)
                if o.dma:
                    ins.then_inc(dsems[e][o.sem_i], 16)
                elif o.inc:
                    ins.then_inc(sems[e], 1)
            for p in pend:
                if p is not None:
                    wait_dma(p)

        for e in ("sp", "pool", "act", "dve", "pe"):
            if self.ops[e]:
                getattr(block, engs[e])(lambda h, e=e: run(e, h))


class Rot:
    def __init__(self, name, aps):
        self.name = name
        self.aps = aps
        self.i = 0

    def next(self):
        i = self.i
        self.i = (i + 1) % len(self.aps)
        return self.aps[i], (self.name, i)


def make_consts():
    c = {}
    c["identf"] = np.eye(128, dtype=np.float32)
    k = np.arange(128)
    same64 = (k[:, None] // 64) == (k[None, :] // 64)
    c["tri64"] = (same64 & (k[:, None] <= k[None, :])).astype(np.float32)
    c["rest64"] = (same64 & (k[:, None] > k[None, :])).astype(np.float32)
    c["blk64"] = same64.astype(np.float32)
    c["sel0"] = np.repeat((k < 64).astype(np.float32)[:, None], 128, 1)
    c["sel1"] = np.repeat((k >= 64).astype(np.float32)[:, None], 128, 1)
    same32 = (k[:, None] < 32) & (k[None, :] < 32)
    c["tri32"] = (same32 & (k[:, None] <= k[None, :])).astype(np.float32)
    c["rest32"] = (same32 & (k[:, None] > k[None, :])).astype(np.float32)
    c["blk32"] = same32.astype(np.float32)
    c["sel32"] = np.repeat((k < 32).astype(np.float32)[:, None], 128, 1)
    lp = np.arange(64)
    c["triu"] = ((k[:, None] % 64) <= lp[None, :]).astype(np.float32)
    c["negm"] = np.where(lp[None, :] >= (k[:, None] % 64), 0.0, -30000.0).astype(np.float32)
    return c


CONST_NAMES = ["identf", "tri64", "rest64", "blk64", "sel0", "sel1", "tri32", "rest32", "blk32", "sel32"]


def build_nc(NT=4, NL=DEPTH):
    nc = bass.Bass("TRN2", target_bir_lowering=False)
    P = Prog()

    def din(name, shape):
        return nc.dram_tensor(name, list(shape), F32, kind="ExternalInput").ap()

    def dout(name, shape):
        return nc.dram_tensor(name, list(shape), F32, kind="ExternalOutput").ap()

    xp = din("xp", [2048, DM])
    xs = din("xs", [4, TS, DM])
    cpool = din("cpool", [DEPTH, 4, 15, DPOOL])
    csconv = din("csconv", [DEPTH, 4, 3, CONV])
    cssd = din("cssd", [DEPTH, 4, 2048, 128])
    cfconv = din("cfconv", [DEPTH, 4, 2, 2 * DFF])
    g_mix_pre = din("norm_mix_pre", [DEPTH, DM])
    w_in = din("w_in", [DEPTH, DM, DIN])
    w_pool = din("w_pool", [DEPTH, 4, 256, 256])
    pool_scale = din("pool_scale", [DEPTH, DPOOL])
    ssd_conv_w = din("ssd_conv_w", [DEPTH, 4, CONV])
    ssd_conv_b = din("ssd_conv_b", [DEPTH, CONV])
    ssd_dt_bias = din("ssd_dt_bias", [DEPTH, 32])
    ssd_a_log = din("ssd_a_log", [DEPTH, 32])
    ssd_d = din("ssd_d", [DEPTH, 32])
    ssd_norm = din("ssd_norm", [DEPTH, DSSD])
    w_out = din("w_out", [DEPTH, 3072, DM])
    g_mix_post = din("norm_mix_post", [DEPTH, DM])
    g_ffn_pre = din("norm_ffn_pre", [DEPTH, DM])
    w_up = din("w_up", [DEPTH, DM, 2 * DFF])
    ffn_conv_w = din("ffn_conv_w", [DEPTH, 3, 2 * DFF])
    ffn_conv_b = din("ffn_conv_b", [DEPTH, 2 * DFF])
    w_down = din("w_down", [DEPTH, DFF, DM])
    g_ffn_post = din("norm_ffn_post", [DEPTH, DM])
    cst = {n: din("c_" + n, [128, 128]) for n in CONST_NAMES}
    c_triu = din("c_triu", [128, 64])
    c_negm = din("c_negm", [128, 64])
    c_rcnt = din("c_rcnt", [128, 8, 16])

    yp = dout("yp", [2048, DM])
    ys = dout("ys", [4, TS, DM])
    opool_p = dout("opool_p", [DEPTH, 15, DPOOL])
    opool_s = dout("opool_s", [DEPTH, 4, 15, DPOOL])
    osconv_p = dout("osconv_p", [DEPTH, 3, CONV])
    osconv_s = dout("osconv_s", [DEPTH, 4, 3, CONV])
    ossd_p = dout("ossd_p", [DEPTH, 2048, 128])
    ossd_s = dout("ossd_s", [DEPTH, 4, 2048, 128])
    ofconv_p = dout("ofconv_p", [DEPTH, 2, 2 * DFF])
    ofconv_s = dout("ofconv_s", [DEPTH, 4, 2, 2 * DFF])
    hscr = nc.dram_tensor("hscr", [DEPTH, 128, 2048], F32).ap()

    st = contextlib.ExitStack()
    with st:
        def sb(name, shape, dt=F32):
            return st.enter_context(nc.sbuf_tensor(name, list(shape), dt))

        x = sb("x", [128, NKD, W])
        xn = sb("xn", [128, NKD, W], BF16)
        hP = sb("hP", [128, 2048])
        hS = sb("hS", [128, 2048])
        hpoolP = sb("hpoolP", [128, DEPTH, 8, 15])
        hsconvP = sb("hsconvP", [128, DEPTH, 20, 3])
        hfconvP = sb("hfconvP", [128, DEPTH, 88, 2])
        hpoolS = sb("hpoolS", [128, 8, 15])
        hsconvS = sb("hsconvS", [128, 20, 3])
        hfconvS = sb("hfconvS", [128, 88, 2])
        gsn = sb("gsn", [128, NKD, 5])
        pscale = sb("pscale", [128, 8, 1])
        scwb = sb("scwb", [128, 20, 5])
        fcwb = sb("fcwb", [128, 88, 4])
        dtb_bc = sb("dtb_bc", [128, 32])
        A_bc = sb("A_bc", [128, 32])
        D_bc = sb("D_bc", [128, 32])
        identf = sb("identf", [128, 128])
        identb = sb("identb", [128, 128], BF16)
        onesb = sb("onesb", [128, 128], BF16)
        cm = {n: sb("cm_" + n, [128, 128]) for n in CONST_NAMES if n != "identf"}
        triu = sb("triu", [128, 64])
        negm = sb("negm", [128, 64])
        rcnt = sb("rcnt", [128, 8, 16])
        epst = sb("epst", [128, 1])
        wslot = [sb("wslot%d" % i, [128, 4096], BF16) for i in range(2)]
        wpl = sb("wpl", [128, 8, 256], BF16)
        wdt = sb("wdt", [128, NKD, 32], BF16)
        sq = [sb("sq%d" % i, [128, W], BF16) for i in range(2)]
        rstd = sb("rstd", [128, W])
        rows = [sb("row%d" % i, [128, 600]) for i in range(5)]
        ssq = sb("ssq", [128, 5, 4])
        grs = sb("grs", [128, 5, 2])
        arena = sb("arena", [128, 21248])
        pw = [st.enter_context(nc.psum_tensor("pw%d" % i, [128, 1024], F32)) for i in range(4)]

        off = [0]

        def carve(words, dt=F32, shape=None):
            a = arena[:, off[0]:off[0] + words]
            off[0] += words
            if dt == BF16:
                a = a.bitcast(BF16)
            return a

        yssd = carve(4352, BF16).rearrange("p (c t) -> p c t", t=W)
        ypool = carve(2176, BF16).rearrange("p (c t) -> p c t", t=W)
        Bfm = carve(544, BF16).rearrange("p (c t) -> p c t", t=W)
        Cfm = carve(544, BF16).rearrange("p (c t) -> p c t", t=W)
        Btm = carve(640, BF16).rearrange("p (b g n) -> p b g n", b=5, g=2)
        dtt = carve(160).rearrange("p (b j) -> p b j", j=32)
        att = carve(160).rearrange("p (b j) -> p b j", j=32)
        acum = carve(160).rearrange("p (b j) -> p b j", j=32)
        nacum = carve(160).rearrange("p (b j) -> p b j", j=32)
        eac = carve(160).rearrange("p (b j) -> p b j", j=32)
        dtdec = carve(160).rearrange("p (b j) -> p b j", j=32)
        cdbc = carve(288).rearrange("p (c j) -> p c j", j=32)
        cbT = carve(640).rearrange("p (b g l) -> p b g l", b=5, g=2)
        xsfm = carve(1088, BF16).rearrange("p (c t) -> p c t", t=W)
        xc = carve(256, BF16)
        xcd = carve(256, BF16)
        zs_off = off[0]
        zs = carve(1280, BF16).rearrange("p (b f) -> p b f", f=512)
        Zb = carve(512)
        Db = carve(512)
        Lb = carve(256, BF16)
        Mb = carve(256, BF16)
        t1 = carve(512)
        t2 = carve(512)
        yg_off = off[0]
        yg = carve(2560, BF16).rearrange("p (b f) -> p b f", f=1024)
        yn = carve(512, BF16)
        Cz = carve(128, BF16).rearrange("p (c l) -> p c l", l=128)
        h16 = carve(256, BF16)
        stage = carve(2048)
        mixer_words = off[0]
        assert mixer_words <= 21248, mixer_words
        hff = arena[:, 0:11968].bitcast(BF16).rearrange("p (c t) -> p c t", t=W)
        stage_f = arena[:, 11968:11968 + 2048]
        wA = [arena[:, 14016 + i * 2048:14016 + (i + 1) * 2048].bitcast(BF16) for i in range(3)]

        STG_ALL = [("stage", g_, j_) for g_ in range(3) for j_ in range(5)]

        def _kl(k):
            return list(k) if isinstance(k, list) else [k]

        AM = "AM"

        def _fl(keys):
            out = []
            for k in keys:
                if isinstance(k, tuple) and len(k) > 0 and k[0] == "multi":
                    out.extend(k[1:])
                else:
                    out.append(k)
            return out

        def A(eng, fn, r=(), w=(), s=(), arena_=True):
            rr = _fl(r) + ([AM] if arena_ else [])
            return P.op(eng, fn, reads=rr, writes=_fl(w), sreads=list(s))

        def Dm(eng, fn, r=(), w=(), arena_=True):
            rr = _fl(r) + ([AM] if arena_ else [])
            return P.op(eng, fn, reads=rr, writes=_fl(w), dma=True)

        def fence():
            P.op("dve", lambda e: e.memset(epst[:, 0:1], EPS), reads=(), writes=[AM, "eps"])

        psW = Rot("psW", [pw[0], pw[1]])
        psB = Rot("psB", [pw[2][:, 0:512], pw[2][:, 512:1024], pw[3][:, 0:512], pw[3][:, 512:1024]])
        rowR = Rot("row", [r_[:, :] for r_ in rows])
        sqR = Rot("sq", [s_[:, :] for s_ in sq])
        wR = Rot("wslot", [w_[:, :] for w_ in wslot])

        class _RotF:
            def __init__(self):
                self.i = 0
                self.items = [(wslot[0][:, :], ("wslot", 0)), (wA[0], ("wA", 0)), (wslot[1][:, :], ("wslot", 1)), (wA[1], ("wA", 1)), (wA[2], ("wA", 2))]

            def next(self):
                it = self.items[self.i]
                self.i = (self.i + 1) % len(self.items)
                return it
        wF = _RotF()
        wX0 = arena[:, zs_off:zs_off + 2048].bitcast(BF16)
        wX1 = arena[:, yg_off:yg_off + 2048].bitcast(BF16)
        WX0_KEYS = ["zs", "Zb", "Db", ("wX", 0)]
        WX1_KEYS = [("yg", b_, h_) for b_ in range(5) for h_ in range(2)] + [("wX", 1)]

        class _RotO:
            def __init__(self):
                self.i = 0
                self.items = [(wslot[0][:, :], ("wslot", 0)), (wX0, ("multi",) + tuple(WX0_KEYS)), (wslot[1][:, :], ("wslot", 1)), (wX1, ("multi",) + tuple(WX1_KEYS))]

            def next(self):
                it = self.items[self.i]
                self.i = (self.i + 1) % len(self.items)
                return it
        wO = _RotO()

        for n in CONST_NAMES:
            dst = identf if n == "identf" else cm[n]
            Dm("sp", lambda e, d=dst, s=cst[n]: e.dma_start(out=d[:], in_=s[:, :]), w=[("c", n)], arena_=False)
        Dm("sp", lambda e: e.dma_start(out=triu[:], in_=c_triu[:, :]), w=["triu"], arena_=False)
        Dm("sp", lambda e: e.dma_start(out=negm[:], in_=c_negm[:, :]), w=["negm"], arena_=False)
        Dm("sp", lambda e: e.dma_start(out=rcnt[:], in_=c_rcnt[:, :, :]), w=["rcnt"], arena_=False)
        Dm("pool", lambda e: e.dma_start(out=identb[:], in_=cst["identf"][:, :]), w=["identb"], arena_=False)
        P.op("dve", lambda e: e.memset(onesb[:], 1.0), writes=["onesb"])
        fence()
        P.op("dve", lambda e: e.memset(hpoolP[:], 0.0), writes=[("hpoolP", l_, c_) for l_ in range(DEPTH) for c_ in range(8)])
        P.op("dve", lambda e: e.memset(hsconvP[:], 0.0), writes=[("hsconvP", l_, c_) for l_ in range(DEPTH) for c_ in range(20)])
        P.op("dve", lambda e: e.memset(hfconvP[:], 0.0), writes=[("hfconvP", l_, c_) for l_ in range(DEPTH) for c_ in range(88)])

        def load_tm_to_fm(src_rows, ntok, col0, key_x):
            Dm("sp", lambda e: e.dma_start(out=stage[0:ntok, :], in_=src_rows), w=STG_ALL)
            for q in range(4):
                ps, pk = psB.next()
                for i in range(4):
                    c = q * 4 + i
                    A("pe", lambda e, ps=ps, c=c, i=i: e.transpose(ps[:, i * 128:i * 128 + ntok], stage[0:ntok, c * 128:(c + 1) * 128], identf[0:ntok, 0:ntok]),
                      r=STG_ALL + [("c", "identf")], w=[pk])
                A("act", lambda e, ps=ps, q=q: e.activation(out=x[:, q * 4:(q + 1) * 4, col0:col0 + ntok],
                                                             in_=ps[:, :].rearrange("p (a b) -> p a b", b=128)[:, :, 0:ntok], func=AF.Copy),
                  r=[pk], w=[key_x])

        def store_fm_to_tm(dst_rows, ntok, col0, stg, skey):
            for q in range(4):
                ps, pk = psB.next()
                for i in range(4):
                    c = q * 4 + i
                    A("pe", lambda e, ps=ps, c=c, i=i: e.transpose(ps[0:ntok, i * 128:(i + 1) * 128], x[:, c, col0:col0 + ntok], identf[:, :]),
                      r=["x", ("c", "identf")], w=[pk])
                A("act", lambda e, ps=ps, q=q: e.activation(out=stg[0:ntok, q * 512:(q + 1) * 512], in_=ps[0:ntok, :], func=AF.Copy),
                  r=[pk], w=_kl(skey))
            Dm("sp", lambda e: e.dma_start(out=dst_rows, in_=stg[0:ntok, :]), r=_kl(skey))

        stg_i = [0]

        def rows_to_fm(srcs, C, dst, dkey):
            if not isinstance(srcs, list):
                srcs = [srcs]
            H = sum(int(a.shape[0]) for a in srcs)
            nch = C // 128
            for p0 in range(0, nch, 16):
                n = min(16, nch - p0)
                gi = stg_i[0]
                stg_i[0] = 0
                pb_ = 32 * gi
                r0 = 0
                rk = []
                for si_, a in enumerate(srcs):
                    h = int(a.shape[0])
                    Dm("sp", lambda e, a=a, r0=r0, h=h: e.dma_start(out=stage[pb_ + r0:pb_ + r0 + h, 0:n * 128], in_=a[:, p0 * 128:(p0 + n) * 128]),
                       w=[("stage", gi, si_)])
                    r0 += h
                ps, pk = psB.next()
                first = True
                for i in range(n):
                    A("pe", lambda e, ps=ps, i=i: e.transpose(ps[:, i * H:(i + 1) * H], stage[pb_:pb_ + H, i * 128:(i + 1) * 128], identf[pb_:pb_ + H, pb_:pb_ + H]),
                      r=[("stage", gi, j_) for j_ in range(5)] + [("c", "identf")], w=[pk])
                A("act", lambda e, ps=ps, p0=p0, n=n: e.activation(out=dst[:, p0:p0 + n, :], in_=ps[:, 0:n * H].rearrange("p (a b) -> p a b", b=H), func=AF.Copy),
                  r=[pk], w=[dkey])

        def fm_to_rows(src, skey, H, C, dst, stg, stkey):
            nch = C // 128
            for p0 in range(0, nch, 16):
                n = min(16, nch - p0)
                for q0 in range(0, n, 4):
                    m = min(4, n - q0)
                    ps, pk = psB.next()
                    for i in range(m):
                        A("pe", lambda e, ps=ps, i=i, c=p0 + q0 + i: e.transpose(ps[0:H, i * 128:(i + 1) * 128], src[:, c, :], identf[:, :]),
                          r=[skey(p0 + q0 + i) if callable(skey) else skey, ("c", "identf")], w=[pk])
                    A("act", lambda e, ps=ps, q0=q0, m=m: e.activation(out=stg[0:H, q0 * 128:(q0 + m) * 128], in_=ps[0:H, 0:m * 128], func=AF.Copy),
                      r=[pk], w=_kl(stkey))
                Dm("sp", lambda e, p0=p0, n=n: e.dma_start(out=dst[:, p0 * 128:(p0 + n) * 128], in_=stg[0:H, 0:n * 128]), r=_kl(stkey))

        def load_w(src, nk, ncols, rot=None):
            slot, wk = (rot or wR).next()
            v = slot[:, 0:nk * ncols].rearrange("p (k f) -> p k f", f=ncols)
            Dm("pool", lambda e: e.dma_start(out=v, in_=src.rearrange("(k p) f -> p k f", p=128)), w=[wk], arena_=(rot is not None))
            return v, wk

        def rmsnorm_stats(src_fn, skeys, nchunks, arena_):
            ps, pk = psW.next()
            for c in range(nchunks):
                s_, sk = sqR.next()
                A("act", lambda e, s_=s_, c=c: e.activation(out=s_, in_=src_fn(c), func=AF.Square), r=skeys, w=[sk], arena_=arena_)
                A("pe", lambda e, s_=s_, c=c, ps=ps: e.matmul(ps[:, 0:512], onesb[:, :], s_[:, 0:512], start=(c == 0), stop=(c == nchunks - 1)),
                  r=[sk, "onesb"], w=[pk], arena_=False)
                A("pe", lambda e, s_=s_, c=c, ps=ps: e.matmul(ps[:, 512:W], onesb[:, :], s_[:, 512:W], start=(c == 0), stop=(c == nchunks - 1)),
                  r=[sk, "onesb"], w=[pk], arena_=False)
            A("act", lambda e, ps=ps: e.activation(out=rstd[:, :], in_=ps[:, 0:W], func=AF.Ln, scale=1.0 / (nchunks * 128), bias=epst[:, 0:1]),
              r=[pk, "eps"], w=["rstd"], arena_=False)
            A("act", lambda e: e.activation(out=rstd[:, :], in_=rstd[:, :], func=AF.Exp, scale=-0.5), r=["rstd"], w=["rstd"], arena_=False)

        def mm_fm(wv, wk, c0, act_fn, akeys, nk, ps, pk, s_off, first=True, last=True, k0=0, arena_=True):
            for k in range(nk):
                A("pe", lambda e, k=k: e.matmul(ps[:, 0:512], wv[:, k, c0:c0 + 128], act_fn(k0 + k)[:, 0:512],
                                                start=(first and k == 0), stop=(last and k == nk - 1)),
                  r=[wk] + akeys, w=[pk], arena_=arena_)
                A("pe", lambda e, k=k: e.matmul(ps[:, s_off:s_off + TS], wv[:, k, c0:c0 + 128], act_fn(k0 + k)[:, 512:W],
                                                start=(first and k == 0), stop=(last and k == nk - 1)),
                  r=[wk] + akeys, w=[pk], arena_=arena_)

        def conv_fm(ps, pk, H, hP_ap, hPk, hS_ap, hSk, wcol, bcol, pkeys, out_fn, arena_):
            E, ek = rowR.next()
            L = 2 * H + W
            A("act", lambda e: e.activation(out=E[:, H:H + W + H], in_=ps[:, 0:W + H], func=AF.Copy), r=[pk], w=[ek], arena_=arena_)
            A("act", lambda e: e.activation(out=E[:, 0:H], in_=hP_ap, func=AF.Copy), r=[hPk], w=[ek], arena_=arena_)
            A("act", lambda e: e.activation(out=E[:, H + 512:2 * H + 512], in_=hS_ap, func=AF.Copy), r=[hSk], w=[ek], arena_=arena_)
            O, ok = rowR.next()
            n = W + H
            A("dve", lambda e: e.tensor_scalar(out=O[:, 0:n], in0=E[:, H:H + n], scalar1=wcol(H), scalar2=bcol, op0=ALU.mult, op1=ALU.add),
              r=[ek] + pkeys, w=[ok], arena_=arena_)
            for i in range(H):
                A("dve", lambda e, i=i: e.scalar_tensor_tensor(out=O[:, 0:n], in0=E[:, i:i + n], scalar=wcol(i), in1=O[:, 0:n], op0=ALU.mult, op1=ALU.add),
                  r=[ek] + pkeys, w=[ok], arena_=arena_)
            A("act", lambda e: e.activation(out=hP_ap, in_=E[:, 512:512 + H], func=AF.Copy), r=[ek], w=[hPk], arena_=arena_)
            A("act", lambda e: e.activation(out=hS_ap, in_=E[:, L - H:L], func=AF.Copy), r=[ek], w=[hSk], arena_=arena_)
            return O, ok

        for ti in range(NT):
            si = ti
            for l in range(NL):
                last_tile = (ti == NT - 1)
                rows_to_fm([g_mix_pre[l:l + 1, :], g_mix_post[l:l + 1, :], g_ffn_pre[l:l + 1, :], g_ffn_post[l:l + 1, :], ssd_norm[l:l + 1, :]], DM, gsn, "gsn")
                rows_to_fm([pool_scale[l:l + 1, :]], DPOOL, pscale, "pscale")
                rows_to_fm([ssd_conv_w[l, :, :], ssd_conv_b[l:l + 1, :]], CONV, scwb, "scwb")
                Dm("sp", lambda e: e.dma_start(out=dtb_bc[:, :], in_=ssd_dt_bias[l, :].partition_broadcast(128)), w=["dtb"], arena_=False)
                Dm("sp", lambda e: e.dma_start(out=A_bc[:, :], in_=ssd_a_log[l, :].partition_broadcast(128)), w=["A"], arena_=False)
                Dm("sp", lambda e: e.dma_start(out=D_bc[:, :], in_=ssd_d[l, :].partition_broadcast(128)), w=["D"], arena_=False)
                A("act", lambda e: e.activation(out=A_bc[:, :], in_=A_bc[:, :], func=AF.Exp), r=["A"], w=["A"], arena_=False)
                Dm("pool", lambda e: e.dma_start(out=wpl[:, :, :], in_=w_pool[l, :, :, :].rearrange("g (h p) d -> p (g h) d", p=128)), w=["wpl"], arena_=False)
                Dm("pool", lambda e: e.dma_start(out=wdt[:, :, :], in_=w_in[l, :, 5632:5664].rearrange("(k p) f -> p k f", p=128)), w=["wdt"], arena_=False)

                if l == 0:
                    for b in range(4):
                        load_tm_to_fm(xp[ti * TP + b * 128: ti * TP + (b + 1) * 128, :], 128, b * 128, "x")
                    load_tm_to_fm(xs[si, :, :], TS, 512, "x")

                if ti == 0:
                    A("dve", lambda e: e.memset(hP[:, :], 0.0), w=[("hP", i) for i in range(4)], arena_=False)
                else:
                    Dm("sp", lambda e: e.dma_start(out=hP[:, :], in_=hscr[l, :, :]), r=[("hscr", l)], w=[("hP", i) for i in range(4)], arena_=False)
                Dm("sp", lambda e: e.dma_start(out=stage.rearrange("p (b n) -> p b n", n=128), in_=cssd[l, si, :, :].rearrange("(b r) n -> r b n", r=128)), w=STG_ALL)
                for q in range(4):
                    ps, pk = psB.next()
                    for i in range(4):
                        b = q * 4 + i
                        A("pe", lambda e, ps=ps, i=i, b=b: e.transpose(ps[:, i * 128:(i + 1) * 128], stage[:, b * 128:(b + 1) * 128], identf[:, :]),
                          r=STG_ALL + [("c", "identf")], w=[pk])
                    A("act", lambda e, ps=ps, q=q: e.activation(out=hS[:, q * 512:(q + 1) * 512], in_=ps[:, :], func=AF.Copy), r=[pk], w=[("hS", q)], arena_=False)
                rows_to_fm([cpool[l, si, :, :]], DPOOL, hpoolS, "hpoolS")
                rows_to_fm([csconv[l, si, :, :]], CONV, hsconvS, "hsconvS")

                rmsnorm_stats(lambda c: x[:, c, :], ["x"], NKD, False)
                for c in range(NKD):
                    A("dve", lambda e, c=c: e.scalar_tensor_tensor(out=xn[:, c, :], in0=x[:, c, :], scalar=gsn[:, c, 0:1], in1=rstd[:, :], op0=ALU.mult, op1=ALU.mult),
                      r=["x", "rstd", "gsn"], w=["xn"], arena_=False)

                xn_fn = lambda k: xn[:, k, :]

                blks = [(0, 128), (128, 128), (256, 128), (384, 128), (512, TS)]
                for b, (c0, nt) in enumerate(blks):
                    ps, pk = psB.next()
                    for k in range(NKD):
                        A("pe", lambda e, k=k, ps=ps, c0=c0, nt=nt: e.matmul(ps[0:nt, 0:32], xn[:, k, c0:c0 + nt], wdt[:, k, :], start=(k == 0), stop=(k == NKD - 1)),
                          r=["xn", "wdt"], w=[pk], arena_=False)
                    if nt < 128:
                        A("dve", lambda e, b=b: e.memset(dtt[:, b, :], 0.0), w=["dtt"])
                    A("dve", lambda e, ps=ps, b=b, nt=nt: e.tensor_tensor(out=dtt[0:nt, b, :], in0=ps[0:nt, 0:32], in1=dtb_bc[0:nt, :], op=ALU.add), r=[pk, "dtb"], w=["dtt"])
                A("act", lambda e: e.activation(out=dtt[:, :, :], in_=dtt[:, :, :], func=AF.Exp), r=["dtt"], w=["dtt"])
                A("act", lambda e: e.activation(out=dtt[:, :, :], in_=dtt[:, :, :], func=AF.Ln, bias=1.0), r=["dtt"], w=["dtt"])
                A("dve", lambda e: e.tensor_scalar(out=dtt[:, 4, :], in0=dtt[:, 4, :], scalar1=cm["sel32"][:, 0:1], scalar2=None, op0=ALU.mult), r=["dtt", ("c", "sel32")], w=["dtt"])
                A("dve", lambda e: e.scalar_tensor_tensor(out=att[:, :, :], in0=dtt[:, :, :], scalar=-1.0, in1=A_bc[:, :].unsqueeze(1).broadcast_to([128, 5, 32]), op0=ALU.mult, op1=ALU.mult),
                  r=["dtt", "A"], w=["att"])
                for b, (c0, nt) in enumerate(blks):
                    tri, rest = ("tri64", "rest64") if b < 4 else ("tri32", "rest32")
                    ps, pk = psB.next()
                    A("pe", lambda e, ps=ps, b=b, tri=tri: e.matmul(ps[:, 0:32], cm[tri][:, :], att[:, b, :], start=True, stop=True), r=["att", ("c", tri)], w=[pk])
                    A("pe", lambda e, ps=ps, b=b, rest=rest: e.matmul(ps[:, 32:64], cm[rest][:, :], att[:, b, :], start=True, stop=True), r=["att", ("c", rest)], w=[pk])
                    sels = ["sel0", "sel1"] if b < 4 else ["sel32"]
                    for ci, sel in enumerate(sels):
                        A("pe", lambda e, ps=ps, b=b, sel=sel, ci=ci: e.matmul(ps[:, 64 + 32 * ci:96 + 32 * ci], cm[sel][:, :], att[:, b, :], start=True, stop=True),
                          r=["att", ("c", sel)], w=[pk])
                    A("act", lambda e, ps=ps, b=b: e.activation(out=acum[:, b, :], in_=ps[:, 0:32], func=AF.Copy), r=[pk], w=["acum"])
                    A("act", lambda e, ps=ps, b=b: e.activation(out=nacum[:, b, :], in_=ps[:, 0:32], func=AF.Copy, scale=-1.0), r=[pk], w=["nacum"])
                    A("act", lambda e, ps=ps, b=b: e.activation(out=eac[:, b, :], in_=ps[:, 0:32], func=AF.Exp), r=[pk], w=["eac"])
                    A("act", lambda e, ps=ps, b=b: e.activation(out=dtdec[:, b, :], in_=ps[:, 32:64], func=AF.Exp), r=[pk], w=["dtdec"])
                    for ci in range(len(sels)):
                        A("act", lambda e, ps=ps, b=b, ci=ci: e.activation(out=cdbc[:, 2 * b + ci, :], in_=ps[:, 64 + 32 * ci:96 + 32 * ci], func=AF.Exp), r=[pk], w=["cdbc"])
                A("dve", lambda e: e.tensor_tensor(out=dtdec[:, :, :], in0=dtdec[:, :, :], in1=dtt[:, :, :], op=ALU.mult), r=["dtdec", "dtt"], w=["dtdec"])

                wv, wk = load_w(w_in[l, :, 5120:5376], NKD, 256, wO)
                wv2, wk2 = load_w(w_in[l, :, 5376:5632], NKD, 256, wO)
                for which, (wv_, wk_, dst, dkey) in enumerate(((wv, wk, Bfm, "Bfm"), (wv2, wk2, Cfm, "Cfm"))):
                    for g in range(2):
                        cidx = 16 + which * 2 + g
                        ps, pk = psW.next()
                        mm_fm(wv_, wk_, g * 128, xn_fn, ["xn"], NKD, ps, pk, 512 + 3, arena_=True)
                        O, ok = conv_fm(ps, pk, 3, hsconvP[:, l, cidx, :], ("hsconvP", l, cidx), hsconvS[:, cidx, :], "hsconvS",
                                        lambda i, cidx=cidx: scwb[:, cidx, i:i + 1], scwb[:, cidx, 4:5], ["scwb"], None, False)
                        A("act", lambda e, O=O, dst=dst, g=g: e.activation(out=dst[:, g, 0:512], in_=O[:, 0:512], func=AF.Silu), r=[ok], w=[dkey])
                        A("act", lambda e, O=O, dst=dst, g=g: e.activation(out=dst[:, g, 512:W], in_=O[:, 515:515 + TS], func=AF.Silu), r=[ok], w=[dkey])
                for b, (c0, nt) in enumerate(blks):
                    ps, pk = psB.next()
                    pb = ps[:, 0:128].bitcast(BF16)
                    for g in range(2):
                        A("pe", lambda e, pb=pb, g=g, c0=c0, nt=nt: e.transpose(pb[0:nt, g * 128:(g + 1) * 128], Bfm[:, g, c0:c0 + nt], identb[:, :]),
                          r=["Bfm", "identb"], w=[pk])
                    A("act", lambda e, pb=pb, b=b, nt=nt: e.activation(out=Btm[0:nt, b, :, :], in_=pb[0:nt, 0:256].rearrange("p (g n) -> p g n", n=128), func=AF.Copy),
                      r=[pk], w=["Btm"])
                    ps2, pk2 = psB.next()
                    for g in range(2):
                        A("pe", lambda e, ps2=ps2, g=g, c0=c0, nt=nt: e.matmul(ps2[0:nt, g * 128:g * 128 + nt], Bfm[:, g, c0:c0 + nt], Cfm[:, g, c0:c0 + nt], start=True, stop=True),
                          r=["Bfm", "Cfm"], w=[pk2])
                    if nt == 128:
                        A("act", lambda e, ps2=ps2, b=b: e.activation(out=cbT[0:64, b, :, :], in_=ps2[0:64, 0:256].rearrange("p (g l) -> p g l", l=128)[:, :, 0:64], func=AF.Copy), r=[pk2], w=["cbT"])
                        A("act", lambda e, ps2=ps2, b=b: e.activation(out=cbT[64:128, b, :, :], in_=ps2[64:128, 0:256].rearrange("p (g l) -> p g l", l=128)[:, :, 64:128], func=AF.Copy), r=[pk2], w=["cbT"])
                    else:
                        A("dve", lambda e, b=b: e.memset(cbT[:, b, :, :], 0.0), w=["cbT"])
                        A("act", lambda e, ps2=ps2, b=b: e.activation(out=cbT[0:TS, b, :, 0:TS], in_=ps2[0:TS, 0:256].rearrange("p (g l) -> p g l", l=128)[:, :, 0:TS], func=AF.Copy), r=[pk2], w=["cbT"])

                for g in range(4):
                    wv, wk = load_w(w_in[l, :, g * 256:(g + 1) * 256], NKD, 256, wO)
                    dts = []
                    for h2 in range(2):
                        c = g * 2 + h2
                        ps, pk = psW.next()
                        mm_fm(wv, wk, h2 * 128, xn_fn, ["xn"], NKD, ps, pk, 512 + 15, arena_=True)
                        E, ek = rowR.next()
                        H = 15
                        A("act", lambda e, E=E, ps=ps: e.activation(out=E[:, H:H + W + H], in_=ps[:, 0:W + H], func=AF.Copy), r=[pk], w=[ek], arena_=False)
                        A("act", lambda e, E=E, c=c: e.activation(out=E[:, 0:H], in_=hpoolP[:, l, c, :], func=AF.Copy), r=[("hpoolP", l, c)], w=[ek], arena_=False)
                        A("act", lambda e, E=E, c=c: e.activation(out=E[:, H + 512:2 * H + 512], in_=hpoolS[:, c, :], func=AF.Copy), r=["hpoolS"], w=[ek], arena_=False)
                        A("act", lambda e, E=E, c=c: e.activation(out=hpoolP[:, l, c, :], in_=E[:, 512:512 + H], func=AF.Copy), r=[ek], w=[("hpoolP", l, c)], arena_=False)
                        A("act", lambda e, E=E, c=c: e.activation(out=hpoolS[:, c, :], in_=E[:, 2 * H + W - H:2 * H + W], func=AF.Copy), r=[ek], w=["hpoolS"], arena_=False)
                        n = W + H
                        cur, ck = E, ek
                        wdw = 1
                        for step in range(g + 1):
                            nxt, nk_ = rowR.next()
                            A("dve", lambda e, cur=cur, nxt=nxt, wdw=wdw: e.tensor_tensor(out=nxt[:, H:H + n], in0=cur[:, H:H + n], in1=cur[:, H - wdw:H - wdw + n], op=ALU.add),
                              r=[ck], w=[nk_], arena_=False)
                            if step < g:
                                A("dve", lambda e, cur=cur, nxt=nxt, wdw=wdw: e.tensor_tensor(out=nxt[:, wdw:H], in0=cur[:, wdw:H], in1=cur[:, 0:H - wdw], op=ALU.add),
                                  r=[ck], w=[nk_], arena_=False)
                            cur, ck = nxt, nk_
                            wdw *= 2
                        wsz = 2 ** (g + 1)
                        dk = ck
                        if ti == 0:
                            A("dve", lambda e, cur=cur, c=c: e.tensor_tensor(out=cur[:, H:H + 16], in0=cur[:, H:H + 16], in1=rcnt[:, c, :], op=ALU.mult), r=[ck, "rcnt"], w=[ck], arena_=False)
                        A("dve", lambda e, cur=cur, E=E, wsz=wsz: e.scalar_tensor_tensor(out=cur[:, H:H + n], in0=cur[:, H:H + n], scalar=1.0 / wsz, in1=E[:, H:H + n], op0=ALU.mult, op1=ALU.subtract),
                          r=[ck, ek], w=[ck], arena_=False)
                        s_, sk = sqR.next()
                        A("act", lambda e, s_=s_, cur=cur: e.activation(out=s_[:, 0:512], in_=cur[:, H:H + 512], func=AF.Copy), r=[dk], w=[sk], arena_=False)
                        A("act", lambda e, s_=s_, cur=cur: e.activation(out=s_[:, 512:W], in_=cur[:, 512 + 2 * H:W + 2 * H], func=AF.Copy), r=[dk], w=[sk], arena_=False)
                        dts.append((s_, sk))
                    for h2 in range(2):
                        c = g * 2 + h2
                        ps, pk = psW.next()
                        for kk in range(2):
                            s_, sk = dts[kk]
                            A("pe", lambda e, ps=ps, s_=s_, kk=kk, h2=h2: e.matmul(ps[:, 0:512], wpl[:, g * 2 + kk, h2 * 128:(h2 + 1) * 128], s_[:, 0:512], start=(kk == 0), stop=(kk == 1)),
                              r=[sk, "wpl"], w=[pk], arena_=False)
                            A("pe", lambda e, ps=ps, s_=s_, kk=kk, h2=h2: e.matmul(ps[:, 512:W], wpl[:, g * 2 + kk, h2 * 128:(h2 + 1) * 128], s_[:, 512:W], start=(kk == 0), stop=(kk == 1)),
                              r=[sk, "wpl"], w=[pk], arena_=False)
                        A("act", lambda e, ps=ps, c=c: e.activation(out=ypool[:, c, :], in_=ps[:, 0:W], func=AF.Identity, scale=pscale[:, c, 0:1]), r=[pk], s=["pscale"], w=["ypool"])

                A("dve", lambda e: e.memset(ssq[:, :, :], 0.0), w=["ssq"], arena_=False)
                for hb in range(4):
                    g = hb // 2
                    for half in range(2):
                        wv, wk = load_w(w_in[l, :, 3072 + hb * 512 + half * 256: 3072 + hb * 512 + (half + 1) * 256], NKD, 256)
                        for h2 in range(2):
                            cc = half * 2 + h2
                            cidx = hb * 4 + cc
                            ps, pk = psW.next()
                            mm_fm(wv, wk, h2 * 128, xn_fn, ["xn"], NKD, ps, pk, 512 + 3, arena_=False)
                            O, ok = conv_fm(ps, pk, 3, hsconvP[:, l, cidx, :], ("hsconvP", l, cidx), hsconvS[:, cidx, :], "hsconvS",
                                            lambda i, cidx=cidx: scwb[:, cidx, i:i + 1], scwb[:, cidx, 4:5], ["scwb"], None, False)
                            A("act", lambda e, O=O, cc=cc: e.activation(out=xsfm[:, cc, 0:512], in_=O[:, 0:512], func=AF.Silu), r=[ok], w=["xsfm"])
                            A("act", lambda e, O=O, cc=cc: e.activation(out=xsfm[:, cc, 512:W], in_=O[:, 515:515 + TS], func=AF.Silu), r=[ok], w=["xsfm"])
                    for half in range(2):
                        wv, wk = load_w(w_in[l, :, 1024 + hb * 512 + half * 256: 1024 + hb * 512 + (half + 1) * 256], NKD, 256)
                        for b, (c0, nt) in enumerate(blks):
                            ps, pk = psB.next()
                            for k in range(NKD):
                                A("pe", lambda e, k=k, ps=ps, c0=c0, nt=nt, wv=wv: e.matmul(ps[0:nt, 0:256], xn[:, k, c0:c0 + nt], wv[:, k, :], start=(k == 0), stop=(k == NKD - 1)),
                                  r=["xn", wk], w=[pk], arena_=False)
                            A("act", lambda e, ps=ps, b=b, nt=nt, half=half: e.activation(out=zs[0:nt, b, half * 256:(half + 1) * 256], in_=ps[0:nt, 0:256], func=AF.Silu), r=[pk], w=["zs"])
                    for b, (c0, nt) in enumerate(blks):
                        isS = (b == 4)
                        q = TS if isS else 64
                        nch = 1 if isS else 2
                        hst, hk = (hS, "hS") if isS else (hP, "hP")
                        j0 = hb * 8
                        v3 = lambda ap: ap.rearrange("p (j d) -> p j d", d=64)
                        ps, pk = psB.next()
                        pb = ps[:, 0:256].bitcast(BF16)
                        for cc in range(4):
                            A("pe", lambda e, pb=pb, cc=cc, c0=c0, nt=nt: e.transpose(pb[0:nt, cc * 128:(cc + 1) * 128], xsfm[:, cc, c0:c0 + nt], identb[:, :]), r=["xsfm", "identb"], w=[pk])
                        A("dve", lambda e: e.tensor_tensor(out=v3(Zb), in0=att[:, b, j0:j0 + 8].unsqueeze(2).broadcast_to([128, 8, 64]),
                                                           in1=triu[:, :].unsqueeze(1).broadcast_to([128, 8, 64]), op=ALU.mult), r=["att", "triu"], w=["Zb"])
                        psr, pkr = psB.next()
                        blkm = "blk32" if isS else "blk64"
                        A("pe", lambda e: e.matmul(psr[:, :], cm[blkm][:, :], Zb, start=True, stop=True), r=["Zb", ("c", blkm)], w=[pkr])
                        A("dve", lambda e: e.tensor_tensor(out=v3(xcd[0:nt, :]), in0=v3(pb[0:nt, :]), in1=dtdec[0:nt, b, j0:j0 + 8].unsqueeze(2).broadcast_to([nt, 8, 64]), op=ALU.mult), r=[pk, "dtdec"], w=["xcd"])
                        A("dve", lambda e: e.tensor_tensor(out=v3(xc[0:nt, :]), in0=v3(pb[0:nt, :]), in1=dtt[0:nt, b, j0:j0 + 8].unsqueeze(2).broadcast_to([nt, 8, 64]), op=ALU.mult), r=[pk, "dtt"], w=["xc"])
                        A("dve", lambda e: e.tensor_tensor(out=v3(t2[0:nt, :]), in0=v3(pb[0:nt, :]), in1=D_bc[0:nt, j0:j0 + 8].unsqueeze(2).broadcast_to([nt, 8, 64]), op=ALU.mult), r=[pk, "D"], w=["t2"])
                        A("dve", lambda e: e.memset(Cz[:, :, :], 0.0), w=["Cz"])
                        for ch in range(nch):
                            A("act", lambda e, ch=ch: e.activation(out=Cz[:, ch, ch * 64:ch * 64 + q], in_=Cfm[:, g, c0 + ch * 64:c0 + ch * 64 + q], func=AF.Copy), r=["Cfm"], w=["Cz"])
                        pso, pko = psB.next()

                        def state_step(ch):
                            p0 = ch * 64
                            A("act", lambda e: e.activation(out=h16[:, :], in_=hst[:, hb * 512:(hb + 1) * 512], func=AF.Copy), r=[(hk, hb)], w=["h16"], arena_=True)
                            A("pe", lambda e: e.matmul(pso[:, :], Cz[:, ch, :], h16[:, :], start=(ch == 0), stop=(ch == nch - 1)), r=["Cz", "h16"], w=[pko])
                            pss, pks = psB.next()
                            A("pe", lambda e: e.matmul(pss[:, :], Btm[p0:p0 + q, b, g, :], xcd[p0:p0 + q, :], start=True, stop=True), r=["Btm", "xcd"], w=[pks])
                            cidx = 2 * b + ch
                            A("dve", lambda e: e.tensor_tensor(out=v3(hst[:, hb * 512:(hb + 1) * 512]), in0=v3(hst[:, hb * 512:(hb + 1) * 512]),
                                                               in1=cdbc[:, cidx, j0:j0 + 8].unsqueeze(2).broadcast_to([128, 8, 64]), op=ALU.mult), r=["cdbc", "h16"], w=[(hk, hb)], arena_=True)
                            A("dve", lambda e: e.tensor_tensor(out=hst[:, hb * 512:(hb + 1) * 512], in0=hst[:, hb * 512:(hb + 1) * 512], in1=pss[:, :], op=ALU.add), r=[pks], w=[(hk, hb)])

                        A("dve", lambda e: e.tensor_tensor(out=v3(Db), in0=v3(psr[:, :]), in1=nacum[:, b, j0:j0 + 8].unsqueeze(2).broadcast_to([128, 8, 64]), op=ALU.add), r=[pkr, "nacum"], w=["Db"])
                        A("dve", lambda e: e.tensor_tensor(out=v3(Db), in0=v3(Db), in1=negm[:, :].unsqueeze(1).broadcast_to([128, 8, 64]), op=ALU.min), r=["Db", "negm"], w=["Db"])
                        A("act", lambda e: e.activation(out=Lb, in_=Db, func=AF.Exp), r=["Db"], w=["Lb"])
                        state_step(0)
                        A("dve", lambda e: e.tensor_tensor(out=v3(Mb), in0=v3(Lb), in1=cbT[:, b, g, :].unsqueeze(1).broadcast_to([128, 8, 64]), op=ALU.mult), r=["Lb", "cbT"], w=["Mb"])
                        psd, pkd = psB.next()
                        for jj in range(8):
                            for ch in range(nch):
                                p0 = ch * 64
                                A("pe", lambda e, jj=jj, p0=p0: e.matmul(psd[p0:p0 + q, jj * 64:(jj + 1) * 64], Mb[p0:p0 + q, jj * 64:jj * 64 + q], xc[p0:p0 + q, jj * 64:(jj + 1) * 64], start=True, stop=True),
                                  r=["Mb", "xc"], w=[pkd])
                        if nch == 2:
                            state_step(1)
                        A("dve", lambda e, pso=pso, nt=nt, b=b: e.tensor_tensor(out=t1[0:nt, :].rearrange("p (j d) -> p j d", d=64), in0=pso[0:nt, :].rearrange("p (j d) -> p j d", d=64),
                                                                          in1=eac[0:nt, b, j0:j0 + 8].unsqueeze(2).broadcast_to([nt, 8, 64]), op=ALU.mult), r=[pko, "eac"], w=["t1"])
                        A("dve", lambda e, psd=psd, nt=nt: e.tensor_tensor(out=t1[0:nt, :], in0=psd[0:nt, :], in1=t1[0:nt, :], op=ALU.add), r=[pkd, "t1"], w=["t1"])
                        A("dve", lambda e, nt=nt: e.tensor_tensor(out=t1[0:nt, :], in0=t1[0:nt, :], in1=t2[0:nt, :], op=ALU.add), r=["t1", "t2"], w=["t1"])
                        hh = hb % 2
                        A("dve", lambda e, nt=nt, b=b, hh=hh: e.tensor_tensor(out=yg[0:nt, b, hh * 512:(hh + 1) * 512], in0=t1[0:nt, :], in1=zs[0:nt, b, :], op=ALU.mult), r=["t1", "zs"], w=[("yg", b, hh)])
                        A("act", lambda e, nt=nt, b=b, hh=hh: e.activation(out=t2[0:nt, :], in_=yg[0:nt, b, hh * 512:(hh + 1) * 512], func=AF.Square, accum_out=ssq[0:nt, b, hb:hb + 1]), r=[("yg", b, hh)], w=["t2", "ssq"])
                        if hh == 1:
                            A("dve", lambda e, nt=nt, b=b: e.tensor_tensor(out=grs[0:nt, b, g:g + 1], in0=ssq[0:nt, b, hb - 1:hb], in1=ssq[0:nt, b, hb:hb + 1], op=ALU.add), r=["ssq"], w=["grs"])
                            A("act", lambda e, nt=nt, b=b: e.activation(out=grs[0:nt, b, g:g + 1], in_=grs[0:nt, b, g:g + 1], func=AF.Ln, scale=1.0 / 1024, bias=epst[0:nt, 0:1]), r=["grs", "eps"], w=["grs"])
                            A("act", lambda e, nt=nt, b=b: e.activation(out=grs[0:nt, b, g:g + 1], in_=grs[0:nt, b, g:g + 1], func=AF.Exp, scale=-0.5), r=["grs"], w=["grs"])
                            A("dve", lambda e, nt=nt, b=b: e.tensor_scalar(out=yn[0:nt, :], in0=yg[0:nt, b, :], scalar1=grs[0:nt, b, g:g + 1], scalar2=None, op0=ALU.mult),
                              r=[("yg", b, 0), ("yg", b, 1)], s=["grs"], w=["yn"])
                            for half in range(2):
                                pst, pkt = psB.next()
                                pbt = pst[:, 0:256].bitcast(BF16)
                                for i in range(4):
                                    fc = half * 4 + i
                                    A("pe", lambda e, pbt=pbt, i=i, fc=fc, nt=nt: e.transpose(pbt[:, i * 128:i * 128 + nt], yn[0:nt, fc * 128:(fc + 1) * 128], identb[0:nt, 0:nt]), r=["yn", "identb"], w=[pkt])
                                A("dve", lambda e, pbt=pbt, half=half, c0=c0, nt=nt: e.tensor_tensor(out=yssd[:, g * 8 + half * 4:g * 8 + half * 4 + 4, c0:c0 + nt],
                                                                                               in0=pbt[:, :].rearrange("p (a t) -> p a t", t=128)[:, :, 0:nt],
                                                                                               in1=gsn[:, g * 8 + half * 4:g * 8 + half * 4 + 4, 4:5].broadcast_to([128, 4, nt]), op=ALU.mult),
                                  r=[pkt, "gsn"], w=["yssd"])

                def store_state(hst, hk, dst):
                    for qd in range(4):
                        ps, pk = psB.next()
                        for i in range(4):
                            bb = qd * 4 + i
                            A("pe", lambda e, ps=ps, i=i, bb=bb: e.transpose(ps[:, i * 128:(i + 1) * 128], hst[:, bb * 128:(bb + 1) * 128], identf[:, :]),
                              r=[(hk, bb // 4), ("c", "identf")], w=[pk])
                        A("act", lambda e, ps=ps, qd=qd: e.activation(out=stage[:, qd * 512:(qd + 1) * 512], in_=ps[:, :], func=AF.Copy), r=[pk], w=STG_ALL)
                    Dm("sp", lambda e: e.dma_start(out=dst.rearrange("(b r) n -> r b n", r=128), in_=stage.rearrange("p (b n) -> p b n", n=128)), r=STG_ALL)

                store_state(hS, "hS", ossd_s[l, si, :, :])
                fm_to_rows(hpoolS, "hpoolS", 15, DPOOL, opool_s[l, si, :, :], stage, STG_ALL)
                fm_to_rows(hsconvS, "hsconvS", 3, CONV, osconv_s[l, si, :, :], stage, STG_ALL)
                if last_tile:
                    store_state(hP, "hP", ossd_p[l, :, :])
                    fm_to_rows(hpoolP[:, l, :, :], lambda c: ("hpoolP", l, c), 15, DPOOL, opool_p[l, :, :], stage, STG_ALL)
                    fm_to_rows(hsconvP[:, l, :, :], lambda c: ("hsconvP", l, c), 3, CONV, osconv_p[l, :, :], stage, STG_ALL)
                elif NT > 1:
                    Dm("sp", lambda e: e.dma_start(out=hscr[l, :, :], in_=hP[:, :]), r=[("hP", i) for i in range(4)], w=[("hscr", l)], arena_=False)

                rows_to_fm([ffn_conv_w[l, :, :], ffn_conv_b[l:l + 1, :]], 2 * DFF, fcwb, "fcwb")
                rows_to_fm([cfconv[l, si, :, :]], 2 * DFF, hfconvS, "hfconvS")

                def mixin(k):
                    return ypool[:, k, :] if k < 8 else yssd[:, k - 8, :]
                for cb in range(8):
                    wv, wk = load_w(w_out[l, :, cb * 256:(cb + 1) * 256].rearrange("(h k) f -> h k f", h=2)[0], 12, 256, wO)
                    wvb, wkb = load_w(w_out[l, :, cb * 256:(cb + 1) * 256].rearrange("(h k) f -> h k f", h=2)[1], 12, 256, wO)
                    for h2 in range(2):
                        c = cb * 2 + h2
                        ps, pk = psW.next()
                        mm_fm(wv, wk, h2 * 128, mixin, ["ypool", "yssd"], 12, ps, pk, 512, first=True, last=False, k0=0)
                        mm_fm(wvb, wkb, h2 * 128, mixin, ["ypool", "yssd"], 12, ps, pk, 512, first=False, last=True, k0=12)
                        A("act", lambda e, ps=ps, c=c: e.activation(out=xn[:, c, :], in_=ps[:, 0:W], func=AF.Copy), r=[pk], w=["xn"], arena_=False)
                rmsnorm_stats(lambda c: xn[:, c, :], ["xn"], NKD, False)
                for c in range(NKD):
                    t_, tk = rowR.next()
                    A("dve", lambda e, c=c, t_=t_: e.scalar_tensor_tensor(out=t_[:, 0:W], in0=xn[:, c, :], scalar=gsn[:, c, 1:2], in1=rstd[:, :], op0=ALU.mult, op1=ALU.mult),
                      r=["xn", "rstd", "gsn"], w=[tk], arena_=False)
                    A("dve", lambda e, c=c, t_=t_: e.tensor_tensor(out=x[:, c, :], in0=x[:, c, :], in1=t_[:, 0:W], op=ALU.add), r=[tk], w=["x"], arena_=False)

                fence()
                rmsnorm_stats(lambda c: x[:, c, :], ["x"], NKD, False)
                for c in range(NKD):
                    A("dve", lambda e, c=c: e.scalar_tensor_tensor(out=xn[:, c, :], in0=x[:, c, :], scalar=gsn[:, c, 2:3], in1=rstd[:, :], op0=ALU.mult, op1=ALU.mult),
                      r=["x", "rstd", "gsn"], w=["xn"], arena_=False)
                for cb in range(22):
                    wg, wgk = load_w(w_up[l, :, cb * 256:(cb + 1) * 256], NKD, 256, wF)
                    wvv, wvk = load_w(w_up[l, :, DFF + cb * 256:DFF + (cb + 1) * 256], NKD, 256, wF)
                    for h2 in range(2):
                        c = cb * 2 + h2
                        outs = []
                        for (wv_, wk_, cidx) in ((wg, wgk, c), (wvv, wvk, 44 + c)):
                            ps, pk = psW.next()
                            mm_fm(wv_, wk_, h2 * 128, xn_fn, ["xn"], NKD, ps, pk, 512 + 2, arena_=True)
                            O, ok = conv_fm(ps, pk, 2, hfconvP[:, l, cidx, :], ("hfconvP", l, cidx), hfconvS[:, cidx, :], "hfconvS",
                                            lambda i, cidx=cidx: fcwb[:, cidx, i:i + 1], fcwb[:, cidx, 3:4], ["fcwb"], None, False)
                            outs.append((O, ok))
                        (Og, ogk), (Ov, ovk) = outs
                        A("act", lambda e, Og=Og: e.activation(out=Og[:, 0:W + 2], in_=Og[:, 0:W + 2], func=AF.Silu), r=[ogk], w=[ogk], arena_=False)
                        A("dve", lambda e, Og=Og, Ov=Ov, c=c: e.tensor_tensor(out=hff[:, c, 0:512], in0=Og[:, 0:512], in1=Ov[:, 0:512], op=ALU.mult), r=[ogk, ovk], w=["hff"])
                        A("dve", lambda e, Og=Og, Ov=Ov, c=c: e.tensor_tensor(out=hff[:, c, 512:W], in0=Og[:, 514:514 + TS], in1=Ov[:, 514:514 + TS], op=ALU.mult), r=[ogk, ovk], w=["hff"])
                fm_to_rows(hfconvS, "hfconvS", 2, 2 * DFF, ofconv_s[l, si, :, :], stage_f, "stage_f")
                if last_tile:
                    fm_to_rows(hfconvP[:, l, :, :], lambda c: ("hfconvP", l, c), 2, 2 * DFF, ofconv_p[l, :, :], stage_f, "stage_f")
                hff_fn = lambda k: hff[:, k, :]
                for c in range(NKD):
                    wa, wak = load_w(w_down[l, 0:2816, c * 128:(c + 1) * 128], 22, 128, wF)
                    wb_, wbk = load_w(w_down[l, 2816:5632, c * 128:(c + 1) * 128], 22, 128, wF)
                    ps, pk = psW.next()
                    mm_fm(wa, wak, 0, hff_fn, ["hff"], 22, ps, pk, 512, first=True, last=False, k0=0)
                    mm_fm(wb_, wbk, 0, hff_fn, ["hff"], 22, ps, pk, 512, first=False, last=True, k0=22)
                    A("act", lambda e, ps=ps, c=c: e.activation(out=xn[:, c, :], in_=ps[:, 0:W], func=AF.Copy), r=[pk], w=["xn"], arena_=False)
                rmsnorm_stats(lambda c: xn[:, c, :], ["xn"], NKD, False)
                for c in range(NKD):
                    t_, tk = rowR.next()
                    A("dve", lambda e, c=c, t_=t_: e.scalar_tensor_tensor(out=t_[:, 0:W], in0=xn[:, c, :], scalar=gsn[:, c, 3:4], in1=rstd[:, :], op0=ALU.mult, op1=ALU.mult),
                      r=["xn", "rstd", "gsn"], w=[tk], arena_=False)
                    A("dve", lambda e, c=c, t_=t_: e.tensor_tensor(out=x[:, c, :], in0=x[:, c, :], in1=t_[:, 0:W], op=ALU.add), r=[tk], w=["x"], arena_=False)
                fence()
                if l == NL - 1:
                    for b in range(4):
                        store_fm_to_tm(yp[ti * TP + b * 128: ti * TP + (b + 1) * 128, :], 128, b * 128, stage, STG_ALL)
                    store_fm_to_tm(ys[si, :, :], TS, 512, stage, STG_ALL)

        P.emit(nc, st)
    return nc


def make_in_maps(inputs, n_cores=8):
    c = make_consts()
    rc = np.ones((128, 8, 16), np.float32)
    for ch in range(8):
        wdw = 2 ** (ch // 2 + 1)
        pos = np.arange(16)
        rc[:, ch, :] = wdw / np.minimum(pos + 1, wdw)
    maps = []
    f = lambda a: np.ascontiguousarray(np.asarray(a, dtype=np.float32))
    shared = {k: f(inputs[k]) for k in ("norm_mix_pre", "w_in", "w_pool", "pool_scale", "ssd_conv_w", "ssd_conv_b", "ssd_dt_bias",
                                        "ssd_a_log", "ssd_d", "ssd_norm", "w_out", "norm_mix_post", "norm_ffn_pre", "w_up",
                                        "ffn_conv_w", "ffn_conv_b", "w_down", "norm_ffn_post")}
    for n in CONST_NAMES:
        shared["c_" + n] = c[n]
    shared["c_triu"] = c["triu"]
    shared["c_negm"] = c["negm"]
    shared["c_rcnt"] = rc
    for i in range(n_cores):
        m = dict(shared)
        m["xp"] = f(inputs["x_prompt"][i])
        m["xs"] = f(inputs["x_sample"][4 * i:4 * i + 4])
        m["cpool"] = f(inputs["cache_pool"][:, 4 * i:4 * i + 4])
        m["csconv"] = f(inputs["state_ssd_conv"][:, 4 * i:4 * i + 4])
        m["cssd"] = f(np.asarray(inputs["state_ssd"])[:, 4 * i:4 * i + 4].reshape(DEPTH, 4, 2048, 128))
        m["cfconv"] = f(inputs["state_ffn_conv"][:, 4 * i:4 * i + 4])
        maps.append(m)
    return maps


def gather(results):
    n = len(results)
    cat = lambda k, ax: np.concatenate([np.asarray(r[k]) for r in results], axis=ax)
    y_p = np.stack([np.asarray(r["yp"]) for r in results], 0)
    y_s = cat("ys", 0)
    pool_p = np.stack([np.asarray(r["opool_p"]) for r in results], 1)
    pool_s = cat("opool_s", 1)
    sconv_p = np.stack([np.asarray(r["osconv_p"]) for r in results], 1)
    sconv_s = cat("osconv_s", 1)
    ssd_p = np.stack([np.asarray(r["ossd_p"]) for r in results], 1).reshape(DEPTH, n, 32, 64, 128)
    ssd_s = cat("ossd_s", 1).reshape(DEPTH, 4 * n, 32, 64, 128)
    fconv_p = np.stack([np.asarray(r["ofconv_p"]) for r in results], 1)
    fconv_s = cat("ofconv_s", 1)
    return tuple(np.ascontiguousarray(a, dtype=np.float32) for a in
                 (y_p, y_s, pool_p, pool_s, sconv_p, sconv_s, ssd_p, ssd_s, fconv_p, fconv_s))


def kernel(**inputs):
    nc = build_nc()
    maps = make_in_maps(inputs, 8)
    res = run_bass_kernel_spmd(nc, maps, core_ids=list(range(8)))
    return gather(res.results)
```
